# Optimizing a Trainium2 kernel written in Bass

```python
import math
import jax, jax.numpy as jnp
from jax import lax
import numpy as np

D_MODEL = 1024
BATCH = 8
SEQ = 8192
DEPTH = 1

RET_HEADS = 4
RET_QK_DIM = 128
RET_V_DIM = 256
RET_CHUNK = 128
RET_QK_WIDTH = RET_HEADS * RET_QK_DIM
RET_V_WIDTH = RET_HEADS * RET_V_DIM
ROPE_BASE = 10000.0
SSM_GROUP_CH = 16
SSM_GROUPS = 32
SSM_WIDTH = SSM_GROUPS * SSM_GROUP_CH
SSM_STATE = 64
DT_MIN = 1e-3
DT_MAX = 1e-1
FFN_HIDDEN = -(-8 * D_MODEL // (3 * 256)) * 256
IN_WIDTH = 2 * RET_QK_WIDTH + 2 * RET_V_WIDTH + SSM_WIDTH + 2 * D_MODEL
IN_SPLITS = (
    RET_QK_WIDTH,
    2 * RET_QK_WIDTH,
    2 * RET_QK_WIDTH + RET_V_WIDTH,
    2 * RET_QK_WIDTH + 2 * RET_V_WIDTH,
    2 * RET_QK_WIDTH + 2 * RET_V_WIDTH + SSM_WIDTH,
)
EPS = 1e-6

kernel_name = "hybrid_retention_s5_gated_block"


def rms_norm(x, g):
    xf = x.astype(jnp.float32)
    xf = xf * lax.rsqrt(jnp.mean(xf * xf, axis=-1, keepdims=True) + EPS)
    return xf.astype(x.dtype) * g


def rotary(x, positions):
    d = x.shape[-1]
    inv_freq = ROPE_BASE ** (-jnp.arange(0, d, 2, dtype=jnp.float32) / d)
    ang = positions.astype(jnp.float32)[..., None] * inv_freq
    cos = jnp.cos(ang)[:, :, None, :].astype(x.dtype)
    sin = jnp.sin(ang)[:, :, None, :].astype(x.dtype)
    x1, x2 = jnp.split(x, 2, axis=-1)
    return jnp.concatenate([x1 * cos - x2 * sin, x1 * sin + x2 * cos], axis=-1)


def retention_chunkwise(q, k, v):
    b, l, h, dk = q.shape
    dv = v.shape[-1]
    c = RET_CHUNK
    n = l // c
    dt = q.dtype
    log_g = jnp.log(1.0 - 2.0 ** (-5.0 - jnp.arange(h, dtype=jnp.float32)))
    idx = jnp.arange(c, dtype=jnp.float32)
    diff = idx[:, None] - idx[None, :]
    inner_decay = jnp.where(diff >= 0, jnp.exp(log_g[:, None, None] * jnp.maximum(diff, 0.0)), 0.0).astype(dt)
    zeta = jnp.exp(log_g[None, :] * (c - 1.0 - idx)[:, None]).astype(dt)
    xi = jnp.exp(log_g[None, :] * (idx + 1.0)[:, None]).astype(dt)
    chunk_decay = jnp.exp(log_g * c).astype(dt)

    qc = q.reshape(b, n, c, h, dk)
    kc = k.reshape(b, n, c, h, dk)
    vc = v.reshape(b, n, c, h, dv)

    scores = jnp.einsum('bnihd,bnjhd->bnhij', qc, kc) * inner_decay[None, None]
    inner = jnp.einsum('bnhij,bnjhe->bnihe', scores, vc)

    kv = jnp.einsum('bnjhd,bnjhe->bnhde', kc, vc * zeta[None, None, :, :, None])

    def step(state, kv_n):
        return chunk_decay[None, :, None, None] * state + kv_n, state

    _, r_prev = lax.scan(step, jnp.zeros_like(kv[:, 0]), jnp.moveaxis(kv, 1, 0))
    r_prev = jnp.moveaxis(r_prev, 0, 1)
    cross = jnp.einsum('bnihd,bnhde->bnihe', qc * xi[None, None, :, :, None], r_prev)
    return (inner + cross).reshape(b, l, h, dv)


def head_group_norm(o):
    of = o.astype(jnp.float32)
    mu = jnp.mean(of, axis=-1, keepdims=True)
    var = jnp.mean(jnp.square(of - mu), axis=-1, keepdims=True)
    return ((of - mu) * lax.rsqrt(var + EPS)).astype(o.dtype)


def s5_mimo(u, a_re, a_im, log_dt, b_re, b_im, c_re, c_im, d_skip):
    bsz, l, _ = u.shape
    ug = u.reshape(bsz, l, SSM_GROUPS, SSM_GROUP_CH)
    dt = jnp.exp(log_dt)[:, None]
    da_re = dt * a_re
    da_im = dt * a_im
    mag = jnp.exp(da_re)
    ab_re = mag * jnp.cos(da_im)
    ab_im = mag * jnp.sin(da_im)
    den = a_re * a_re + a_im * a_im
    num_re = ab_re - 1.0
    f_re = (num_re * a_re + ab_im * a_im) / den
    f_im = (ab_im * a_re - num_re * a_im) / den
    bb_re = f_re[:, :, None] * b_re - f_im[:, :, None] * b_im
    bb_im = f_re[:, :, None] * b_im + f_im[:, :, None] * b_re
    bu_re = jnp.einsum('gpc,blgc->blgp', bb_re, ug)
    bu_im = jnp.einsum('gpc,blgc->blgp', bb_im, ug)
    shape_a = (1, l, SSM_GROUPS, SSM_STATE)
    a_seq_re = jnp.broadcast_to(ab_re[None, None], shape_a)
    a_seq_im = jnp.broadcast_to(ab_im[None, None], shape_a)

    def combine(e1, e2):
        a1r, a1i, b1r, b1i = e1
        a2r, a2i, b2r, b2i = e2
        return (a2r * a1r - a2i * a1i,
                a2r * a1i + a2i * a1r,
                a2r * b1r - a2i * b1i + b2r,
                a2r * b1i + a2i * b1r + b2i)

    _, _, x_re, x_im = lax.associative_scan(combine, (a_seq_re, a_seq_im, bu_re, bu_im), axis=1)
    y = jnp.einsum('gcp,blgp->blgc', c_re, x_re) - jnp.einsum('gcp,blgp->blgc', c_im, x_im)
    y = y + d_skip.reshape(SSM_GROUPS, SSM_GROUP_CH) * ug
    return y.reshape(bsz, l, SSM_WIDTH)


def setup_inputs(seed: int = 0) -> dict:
    key = jax.random.key(seed)
    ks = jax.random.split(key, 24)
    f32 = jnp.float32

    def nrm(k, shape, scale):
        return jax.random.normal(k, shape, f32) * scale

    def gain(k):
        return 1.0 + 0.02 * jax.random.normal(k, (DEPTH, D_MODEL), f32)

    x = jax.random.normal(ks[0], (BATCH, SEQ, D_MODEL), f32)
    offset = jax.random.randint(ks[1], (BATCH,), 0, 4096, dtype=jnp.int32)
    positions = offset[:, None] + jnp.arange(SEQ, dtype=jnp.int32)[None, :]
    n_idx = jnp.arange(SSM_STATE, dtype=f32)
    ssm_a_re = -0.5 + 0.01 * jax.random.normal(ks[2], (DEPTH, SSM_GROUPS, SSM_STATE), f32)
    ssm_a_im = jnp.broadcast_to(math.pi * n_idx, (DEPTH, SSM_GROUPS, SSM_STATE))
    ssm_log_dt = jax.random.uniform(ks[3], (DEPTH, SSM_GROUPS), f32, math.log(DT_MIN), math.log(DT_MAX))
    b_scale = (2.0 * SSM_GROUP_CH) ** -0.5
    c_scale = (2.0 * SSM_STATE) ** -0.5
    return {
        "x": x,
        "positions": positions,
        "mix_pre_norm": gain(ks[4]),
        "w_in": nrm(ks[5], (DEPTH, D_MODEL, IN_WIDTH), D_MODEL ** -0.5),
        "ssm_a_re": ssm_a_re,
        "ssm_a_im": ssm_a_im,
        "ssm_log_dt": ssm_log_dt,
        "ssm_b_re": nrm(ks[6], (DEPTH, SSM_GROUPS, SSM_STATE, SSM_GROUP_CH), b_scale),
        "ssm_b_im": nrm(ks[7], (DEPTH, SSM_GROUPS, SSM_STATE, SSM_GROUP_CH), b_scale),
        "ssm_c_re": nrm(ks[8], (DEPTH, SSM_GROUPS, SSM_GROUP_CH, SSM_STATE), c_scale),
        "ssm_c_im": nrm(ks[9], (DEPTH, SSM_GROUPS, SSM_GROUP_CH, SSM_STATE), c_scale),
        "ssm_d": nrm(ks[10], (DEPTH, SSM_WIDTH), 1.0),
        "w_glu_val": nrm(ks[11], (DEPTH, SSM_WIDTH, D_MODEL), SSM_WIDTH ** -0.5),
        "w_glu_gate": nrm(ks[12], (DEPTH, SSM_WIDTH, D_MODEL), SSM_WIDTH ** -0.5),
        "w_ret_up": nrm(ks[13], (DEPTH, RET_V_WIDTH, D_MODEL), RET_V_WIDTH ** -0.5),
        "w_out": nrm(ks[14], (DEPTH, D_MODEL, D_MODEL), D_MODEL ** -0.5),
        "mix_post_norm": gain(ks[15]),
        "ffn_pre_norm": gain(ks[16]),
        "w_ffn_gate": nrm(ks[17], (DEPTH, D_MODEL, FFN_HIDDEN), D_MODEL ** -0.5),
        "w_ffn_up": nrm(ks[18], (DEPTH, D_MODEL, FFN_HIDDEN), D_MODEL ** -0.5),
        "w_ffn_down": nrm(ks[19], (DEPTH, FFN_HIDDEN, D_MODEL), FFN_HIDDEN ** -0.5),
        "ffn_post_norm": gain(ks[20]),
    }


def reference(x, positions, mix_pre_norm, w_in, ssm_a_re, ssm_a_im, ssm_log_dt, ssm_b_re, ssm_b_im,
              ssm_c_re, ssm_c_im, ssm_d, w_glu_val, w_glu_gate, w_ret_up, w_out, mix_post_norm,
              ffn_pre_norm, w_ffn_gate, w_ffn_up, w_ffn_down, ffn_post_norm):
    bsz, l, _ = x.shape
    h = x
    for layer in range(DEPTH):
        u = rms_norm(h, mix_pre_norm[layer])
        proj = u @ w_in[layer]
        q, k, v, g_ret, u_ssm, g_merge = jnp.split(proj, IN_SPLITS, axis=-1)

        q = rotary(q.reshape(bsz, l, RET_HEADS, RET_QK_DIM), positions)
        k = rotary(k.reshape(bsz, l, RET_HEADS, RET_QK_DIM), positions) * (RET_QK_DIM ** -0.5)
        v = v.reshape(bsz, l, RET_HEADS, RET_V_DIM)
        ret = head_group_norm(retention_chunkwise(q, k, v)).reshape(bsz, l, RET_V_WIDTH)
        y_a = (jax.nn.silu(g_ret) * ret) @ w_ret_up[layer]

        y_s = s5_mimo(u_ssm, ssm_a_re[layer], ssm_a_im[layer], ssm_log_dt[layer], ssm_b_re[layer],
                      ssm_b_im[layer], ssm_c_re[layer], ssm_c_im[layer], ssm_d[layer])
        z = jax.nn.gelu(y_s)
        y_b = (z @ w_glu_val[layer]) * jax.nn.sigmoid(z @ w_glu_gate[layer])

        gate_a, gate_b = jnp.split(jax.nn.sigmoid(g_merge), 2, axis=-1)
        mixed = (gate_a * y_a + gate_b * y_b) @ w_out[layer]
        h = h + rms_norm(mixed, mix_post_norm[layer])

        f_in = rms_norm(h, ffn_pre_norm[layer])
        f = (jax.nn.silu(f_in @ w_ffn_gate[layer]) * (f_in @ w_ffn_up[layer])) @ w_ffn_down[layer]
        h = h + rms_norm(f, ffn_post_norm[layer])
    return h
```

```python
import math
from contextlib import ExitStack

import numpy as np
import concourse.bass as bass
import concourse.mybir as mybir
from concourse.bass_utils import run_bass_kernel_spmd

F32 = mybir.dt.float32
BF16 = mybir.dt.bfloat16
I32 = mybir.dt.int32
AF = mybir.ActivationFunctionType
ALU = mybir.AluOpType
AX = mybir.AxisListType

D = 1024
SEQ = 8192
NH = 4
INW = 5632
FFN = 2816
EPS = 1e-6
T2 = 64
TWO_PI = 2.0 * math.pi
C1 = 6.28125
C2 = TWO_PI - C1
MAGIC = 12582912.0
GAM = [1.0 - 2.0 ** (-5.0 - h) for h in range(NH)]
GAM128 = [g ** 128 for g in GAM]
SAME_DIST = 4
S5_P3_DELAY = 1.5
WIN_ORDER = [6, 0, 1, 2, 3, 4, 5, 7, 8, 9, 10]


class _Op:
    __slots__ = ("eng", "meth", "kw", "r", "w", "dma", "deps", "need", "sem", "val", "raw", "pos", "vt")


class Prog:
    def __init__(self, nc, es):
        self.nc = nc
        self.es = es
        self.ops = []
        self.engs = {"pe": nc.tensor, "act": nc.scalar, "dve": nc.vector, "pool": nc.gpsimd, "sp": nc.sync}
        self.clock = 0.0
        self.override = None
        self.pe_delay = 0.0

    def add(self, eng, meth, r=(), w=(), dma=None, **kw):
        o = _Op()
        o.eng, o.meth, o.kw, o.r, o.w, o.dma = eng, meth, kw, tuple(r), tuple(w), dma
        if self.override is not None:
            o.vt = self.override[0] + (self.pe_delay if eng == "pe" else 0.0)
            self.override[0] += self.override[1]
        else:
            if eng == "pe":
                if meth == "matmul":
                    n = 1
                    for d_ in kw["rhs"].shape[1:]:
                        n *= d_
                    self.clock += n / 2400.0 + 0.02
                else:
                    self.clock += 0.07
            o.vt = self.clock
        self.ops.append(o)
        return o

    def finalize(self):
        import heapq
        nc = self.nc
        ops = self.ops
        n = len(ops)
        last_w = {}
        readers = {}
        for i, o in enumerate(ops):
            deps = set()
            for k in o.r:
                if k in last_w:
                    deps.add(last_w[k])
            for k in o.w:
                if k in last_w:
                    deps.add(last_w[k])
                deps.update(readers.get(k, ()))
            deps.discard(i)
            o.deps = deps
            o.raw = set(last_w[k] for k in o.r if k in last_w)
            for k in o.r:
                readers.setdefault(k, []).append(i)
            for k in o.w:
                last_w[k] = i
                readers[k] = []
            o.need = False
        succ = [[] for _ in range(n)]
        indeg = [0] * n
        for i, o in enumerate(ops):
            indeg[i] = len(o.deps)
            for d in o.deps:
                succ[d].append(i)
        heap = [(ops[i].vt, i) for i in range(n) if indeg[i] == 0]
        heapq.heapify(heap)
        order = []
        while heap:
            _, i = heapq.heappop(heap)
            order.append(i)
            for j in succ[i]:
                indeg[j] -= 1
                if indeg[j] == 0:
                    heapq.heappush(heap, (max(ops[j].vt, ops[i].vt), j))
        assert len(order) == n
        epos = {}
        rank = [0] * n
        for r_, i in enumerate(order):
            o = ops[i]
            rank[i] = r_
            o.pos = epos.get(o.eng, 0)
            epos[o.eng] = o.pos + 1
        for i in order:
            o = ops[i]
            best = {}
            for d in o.deps:
                p = ops[d]
                if p.dma is None and p.eng == o.eng:
                    if not (o.eng in ("dve", "act", "pool") and o.dma is None and d in o.raw and (o.pos - p.pos) <= SAME_DIST):
                        continue
                k = ("d", p.dma) if p.dma is not None else ("e", p.eng)
                if k not in best or rank[d] > rank[best[k]]:
                    best[k] = d
            o.deps = set(best.values())
            for d in o.deps:
                ops[d].need = True
        sems = {}

        def getsem(name):
            if name not in sems:
                sems[name] = self.es.enter_context(nc.semaphore(name.replace(".", "_")))
            return sems[name]

        cnt = {}
        for i in order:
            o = ops[i]
            if o.dma is not None:
                nm = "d_" + o.dma
                cnt[nm] = cnt.get(nm, 0) + 16
                o.sem, o.val = nm, cnt[nm]
            elif o.need:
                nm = "e_" + o.eng
                cnt[nm] = cnt.get(nm, 0) + 1
                o.sem, o.val = nm, cnt[nm]
            else:
                o.sem, o.val = None, 0
        waited = {e: {} for e in self.engs}
        for i in order:
            o = ops[i]
            e = self.engs[o.eng]
            need = {}
            for d in o.deps:
                p = ops[d]
                if p.val > need.get(p.sem, 0):
                    need[p.sem] = p.val
            for s_, v in need.items():
                if waited[o.eng].get(s_, 0) < v:
                    e.wait_ge(getsem(s_), v)
                    waited[o.eng][s_] = v
            ins = getattr(e, o.meth)(**o.kw)
            if o.sem is not None and (o.dma is not None or o.need):
                ins.then_inc(getsem(o.sem), 16 if o.dma is not None else 1)
        for s_, v in cnt.items():
            if s_.startswith("d_"):
                nc.sync.wait_ge(getsem(s_), v)


def build(ntok, dbg=()):
    assert ntok % 512 == 0
    NB = ntok // 512
    NT = ntok // 128
    nc = bass.Bass("TRN2", target_bir_lowering=False)

    def din(name, shape, dt=F32):
        return nc.dram_tensor(name, list(shape), dt, kind="ExternalInput").ap()

    x_d = din("x", [ntok, D])
    pos_d = din("pos", [128, NT], I32)
    w_in_d = din("w_in", [D, INW])
    w_ru_d = din("w_ret_up", [D, D])
    w_gv_d = din("w_glu_val", [512, D])
    w_gg_d = din("w_glu_gate", [512, D])
    w_out_d = din("w_out", [D, D])
    w_fg_d = din("w_ffn_gate", [D, FFN])
    w_fu_d = din("w_ffn_up", [D, FFN])
    w_fd_d = din("w_ffn_down", [FFN, D])
    gpreT_d = din("g_pre_T", [128, 8])
    gffnT_d = din("g_ffn_T", [128, 8])
    gpost_d = din("g_post", [1, D])
    gfpost_d = din("g_fpost", [1, D])
    are_d = din("a_re", [128, 16])
    aim_d = din("a_im", [128, 16])
    ldt_d = din("log_dt", [128, 16])
    bre_d = din("b_re", [128, 16, 16])
    bim_d = din("b_im", [128, 16, 16])
    cre_d = din("ct_re", [128, 16, 16])
    cim_d = din("ct_im", [128, 16, 16])
    dT_d = din("d_T", [128, 4])
    identf_d = din("ident_f", [128, 128])
    maskT_d = din("maskT", [128, 128])
    qs_d = din("qscale", [128, 4])
    ks_d = din("kscale", [128, 4])
    invf_d = din("invfreq", [128, 64])
    y_d = nc.dram_tensor("y", [ntok, D], F32, kind="ExternalOutput").ap()
    dbg_d = {}
    for nm, shape in dbg:
        dbg_d[nm] = nc.dram_tensor("dbg_" + nm, list(shape), F32, kind="ExternalOutput").ap()

    es = ExitStack()
    with es:
        def sb(name, shape, dt=F32):
            return es.enter_context(nc.sbuf_tensor("s_" + name, list(shape), dt))

        def ps(name, shape, dt=F32):
            return es.enter_context(nc.psum_tensor("p_" + name, list(shape), dt))

        ident_f = sb("ident_f", [128, 128])
        ident_b = sb("ident_b", [128, 128], BF16)
        maskT = sb("maskT", [128, 128])
        qscale = sb("qscale", [128, 4])
        kscale = sb("kscale", [128, 4])
        invf = sb("invf", [128, 64])
        gpreT = sb("gpreT", [128, 8])
        gffnT = sb("gffnT", [128, 8])
        gpost = sb("gpost", [128, D])
        gfpost = sb("gfpost", [128, D])
        pos_i = sb("pos_i", [128, NT], I32)
        pos_f = sb("pos_f", [128, NT])
        a_re = sb("a_re", [128, 16]); a_im = sb("a_im", [128, 16]); ldt = sb("ldt", [128, 16])
        d_T = sb("d_T", [128, 4])
        Lbb = sb("Lbb", [128, 16, 2, 128], BF16)
        CC = sb("CC", [128, 16, 2, 128], BF16)
        Ctab = sb("Ctab", [128, 16, T2]); Stab = sb("Stab", [128, 16, T2]); R0 = sb("R0", [128, 16, T2])
        rdec = sb("rdec", [128, 16])
        car_re = sb("car_re", [128, 16]); car_im = sb("car_im", [128, 16])
        sm = [sb("sm%d" % i, [128, 16]) for i in range(12)]
        Rs = sb("Rs", [128, NH, 256]); R_bf = sb("R_bf", [128, NH, 256], BF16)
        x_sb = sb("x_sb", [128, 4, D])
        xflat = x_sb[:].rearrange("p a b -> p (a b)")
        _v = lambda k: xflat[:, k * 256:(k + 1) * 256].rearrange("p (a b) -> p a b", a=16)
        b_re, b_im, ct_re, ct_im, bb_re, bb_im, bt1, bt2 = [_v(k) for k in range(8)]
        bexp = xflat[:, 2048:2304].rearrange("p (a b) -> p a b", a=2)
        actT = sb("actT", [128, 8, 512], BF16)
        reg1 = sb("reg1", [128, 12288], BF16)
        qk_sb = reg1[:, 0:4096].rearrange("p (a b) -> p a b", a=4)
        v_sb = reg1[:, 4096:8192].rearrange("p (a b) -> p a b", a=4)
        sg_sb = reg1[:, 8192:12288].rearrange("p (a b) -> p a b", a=4)
        mix_sb = sg_sb
        hidT = reg1[:, 0:11264].rearrange("p (a b) -> p a b", a=22)
        reg2 = sb("reg2", [128, 8192], BF16)
        gates = reg2[:, :].rearrange("p (a b) -> p a b", a=4)
        stg = reg2[:, :].bitcast(F32).rearrange("p (a b) -> p a b", a=4)
        ussmT = sb("ussmT", [128, 4, 512], BF16)
        zT = sb("zT", [128, 4, 512], BF16)
        xn = sb("xn", [128, D], BF16)
        junk = sb("junk", [128, D], BF16)
        ss = sb("ss", [128, 8])
        rinv = sb("rinv", [128, 8])
        ss2 = sb("ss2", [128, 8])
        halfpi = sb("halfpi", [128, 1])
        epst = sb("epst", [128, 1])
        cs_t = sb("cs_t", [128, 4, 2, 64])
        ang2 = sb("ang2", [128, 64]); ang = sb("ang", [128, 64]); kf = sb("kf", [128, 64]); rr = sb("rr", [128, 64]); ab = sb("ab", [128, 64])
        rt = [sb("rt%d" % i, [128, 4, 64]) for i in range(4)]
        ro = sb("ro", [128, 4, 2, 64])
        qkT = sb("qkT", [128, 8, 128], BF16)
        sc_bf = sb("sc_bf", [128, NH, 128], BF16)
        retn = sb("retn", [128, D])
        gated = sb("gated", [128, D], BF16)
        st6 = sb("st6", [128, NH, 6]); mv = sb("mv", [128, NH, 2]); rstd = sb("rstd", [128, NH])
        w_re = sb("w_re", [128, 8, T2]); w_im = sb("w_im", [128, 8, T2])
        w_re2 = sb("w_re2", [128, 8, T2]); w_im2 = sb("w_im2", [128, 8, T2])
        s1 = sb("s1", [128, 8, T2]); s2 = sb("s2", [128, 8, T2])
        xr_bf = sb("xr_bf", [128, 16, T2], BF16); xi_bf = sb("xi_bf", [128, 16, T2], BF16)
        cl = [sb("cl%d" % i, [128, 8]) for i in range(4)]
        p1 = sb("p1", [128, 8, T2]); p2 = sb("p2", [128, 8, T2])
        RC = sb("RC", [128, 16]); RS = sb("RS", [128, 16])
        qkraw = sb("qkraw", [128, 512])

        yt = sb("yt", [128, 4, T2]); gl1 = sb("gl1", [128, 4, T2]); gl2 = sb("gl2", [128, 4, T2])
        ttsgt = sb("ttsgt", [128, 1024])
        tt = ttsgt[:, 0:512]
        sgt = ttsgt[:, 512:1024]
        qkT32 = ttsgt[:, :].rearrange("p (a b) -> p a b", a=8)
        qk32 = retn
        NRING = 4
        ring = [sb("ring%d" % i, [128, 8, 512], BF16) for i in range(NRING)]
        tok1 = sb("tok1", [128, 1])
        pT = [ps("pT%d" % i, [128, 1024], BF16) for i in range(2)]
        pA = [ps("pA%d" % i, [128, 512]) for i in range(6)]

        es.enter_context(nc.Block())
        P = Prog(nc, es)
        st = {"pt": 0, "pa": 0, "ring": 0, "ps": 0}

        def next_pT():
            i = st["pt"]; st["pt"] = (i + 1) % 2
            return pT[i], "pT%d" % i

        def next_pA():
            i = st["pa"]; st["pa"] = (i + 1) % 4
            return pA[i], "pA%d" % i

        def next_pS():
            i = 4 + st["ps"]; st["ps"] = (st["ps"] + 1) % 2
            return pA[i], "pA%d" % i

        def pool(meth, r, w, **kw):
            return P.add("pool", meth, r=r, w=w, **kw)

        def load(dst, src, key, eng="sp"):
            P.add(eng, "dma_start", w=[key], dma="c_" + key, out=dst, in_=src)

        load(ident_f[:], identf_d[:, :], "ident_f")
        load(maskT[:], maskT_d[:, :], "maskT")
        load(qscale[:], qs_d[:, :], "qscale")
        load(kscale[:], ks_d[:, :], "kscale")
        load(invf[:], invf_d[:, :], "invf")
        load(gpreT[:], gpreT_d[:, :], "gpreT")
        load(gffnT[:], gffnT_d[:, :], "gffnT")
        load(gpost[:], gpost_d.partition_broadcast(128), "gpost")
        load(gfpost[:], gfpost_d.partition_broadcast(128), "gfpost")
        load(pos_i[:], pos_d[:, :], "pos_i")
        load(a_re[:], are_d[:, :], "a_re"); load(a_im[:], aim_d[:, :], "a_im"); load(ldt[:], ldt_d[:, :], "ldt")
        load(b_re[:], bre_d[:, :, :], "b_re"); load(b_im[:], bim_d[:, :, :], "b_im")
        load(ct_re[:], cre_d[:, :, :], "ct_re"); load(ct_im[:], cim_d[:, :, :], "ct_im")
        load(d_T[:], dT_d[:, :], "d_T")

        def dve(meth, r, w, **kw):
            return P.add("dve", meth, r=r, w=w, **kw)

        def act(r, w, out, in_, func, **kw):
            return P.add("act", "activation", r=r, w=w, out=out, in_=in_, func=func, **kw)

        dve("tensor_copy", ["ident_f"], ["ident_b"], out=ident_b[:], in_=ident_f[:])
        dve("tensor_copy", ["pos_i"], ["pos_f"], out=pos_f[:], in_=pos_i[:])
        dve("memset", [], ["Rs.%d" % h for h in range(NH)], ap=Rs[:], constant=0.0)
        dve("memset", [], ["R_bf.%d" % h for h in range(NH)], ap=R_bf[:], constant=0.0)
        dve("memset", [], ["car0", "car1"], ap=car_re[:], constant=0.0)
        dve("memset", [], ["car0", "car1"], ap=car_im[:], constant=0.0)

        dt_, dare, daim, mag, sn, cs, abr, abi, den, fre, fim, tmp = sm

        def sincos(n, theta_ap, theta_keys, sin_out, cos_out, out_keys):
            A, K_, R_, B_ = ang[:, 0:n], kf[:, 0:n], rr[:, 0:n], ab[:, 0:n]
            dve("tensor_scalar", theta_keys, ["sc_k"], out=K_, in0=theta_ap, scalar1=1.0 / TWO_PI, scalar2=MAGIC, op0=ALU.mult, op1=ALU.add)
            dve("tensor_scalar", ["sc_k"], ["sc_k"], out=K_, in0=K_, scalar1=-MAGIC, scalar2=None, op0=ALU.add)
            dve("scalar_tensor_tensor", ["sc_k"] + theta_keys, ["sc_r"], out=R_, in0=K_, scalar=-C1, in1=theta_ap, op0=ALU.mult, op1=ALU.add)
            dve("scalar_tensor_tensor", ["sc_k", "sc_r"], ["sc_r"], out=R_, in0=K_, scalar=-C2, in1=R_, op0=ALU.mult, op1=ALU.add)
            dve("tensor_scalar", ["sc_r"], ["sc_r"], out=R_, in0=R_, scalar1=math.pi, scalar2=-math.pi, op0=ALU.min, op1=ALU.max)
            dve("scalar_tensor_tensor", ["sc_r"], ["sc_b"], out=B_, in0=R_, scalar=-1.0, in1=R_, op0=ALU.mult, op1=ALU.max)
            act(["sc_r"], out_keys, sin_out, R_, AF.Sin)
            act(["sc_b", "halfpi"], out_keys, cos_out, B_, AF.Sin, scale=-1.0, bias=halfpi[:, 0:1])

        dve("memset", [], ["halfpi"], ap=halfpi[:], constant=math.pi / 2)
        dve("memset", [], ["epst"], ap=epst[:], constant=EPS)

        def rsqrt(out_ap, in_ap, scale, rkeys, wkeys):
            act(rkeys + ["epst"], wkeys, out_ap, in_ap, AF.Sqrt, scale=scale, bias=epst[:, 0:1])
            dve("reciprocal", wkeys, wkeys, out=out_ap, in_=out_ap)
        act(["ldt"], ["dt"], dt_[:], ldt[:], AF.Exp)
        dve("tensor_tensor", ["dt", "a_re"], ["dare"], out=dare[:], in0=dt_[:], in1=a_re[:], op=ALU.mult)
        dve("tensor_tensor", ["dt", "a_im"], ["daim"], out=daim[:], in0=dt_[:], in1=a_im[:], op=ALU.mult)
        act(["dare"], ["mag"], mag[:], dare[:], AF.Exp)
        sincos(16, daim[:], ["daim"], sn[:], cs[:], ["sncs"])
        for nm_, ap_, k_ in [("d_daim", daim[:], ["daim"]), ("d_kf", kf[:, 0:16], ["sc_k"]), ("d_rr", rr[:, 0:16], ["sc_r"]), ("d_ab", ab[:, 0:16], ["sc_b"]), ("d_sn", sn[:], ["sncs"]), ("d_cs", cs[:], ["sncs"])]:
            if nm_ in dbg_d:
                P.add("pool", "dma_start", r=k_, dma="dbg_" + nm_, out=dbg_d[nm_], in_=ap_)
        SN, CS = "sncs", "sncs"
        dve("tensor_tensor", ["mag", CS], ["abr"], out=abr[:], in0=mag[:], in1=cs[:], op=ALU.mult)
        dve("tensor_tensor", ["mag", SN], ["abi"], out=abi[:], in0=mag[:], in1=sn[:], op=ALU.mult)
        dve("tensor_tensor", ["a_re"], ["den"], out=den[:], in0=a_re[:], in1=a_re[:], op=ALU.mult)
        dve("tensor_tensor", ["a_im"], ["tmp"], out=tmp[:], in0=a_im[:], in1=a_im[:], op=ALU.mult)
        dve("tensor_tensor", ["den", "tmp"], ["den"], out=den[:], in0=den[:], in1=tmp[:], op=ALU.add)
        dve("reciprocal", ["den"], ["den"], out=den[:], in_=den[:])
        dve("tensor_scalar", ["abr"], ["numre"], out=dare[:], in0=abr[:], scalar1=-1.0, scalar2=None, op0=ALU.add)
        dve("tensor_tensor", ["numre", "a_re"], ["fre"], out=fre[:], in0=dare[:], in1=a_re[:], op=ALU.mult)
        dve("tensor_tensor", ["abi", "a_im"], ["tmp"], out=tmp[:], in0=abi[:], in1=a_im[:], op=ALU.mult)
        dve("tensor_tensor", ["fre", "tmp"], ["fre"], out=fre[:], in0=fre[:], in1=tmp[:], op=ALU.add)
        dve("tensor_tensor", ["fre", "den"], ["fre"], out=fre[:], in0=fre[:], in1=den[:], op=ALU.mult)
        dve("tensor_tensor", ["abi", "a_re"], ["fim"], out=fim[:], in0=abi[:], in1=a_re[:], op=ALU.mult)
        dve("tensor_tensor", ["numre", "a_im"], ["tmp"], out=tmp[:], in0=dare[:], in1=a_im[:], op=ALU.mult)
        dve("tensor_tensor", ["fim", "tmp"], ["fim"], out=fim[:], in0=fim[:], in1=tmp[:], op=ALU.subtract)
        dve("tensor_tensor", ["fim", "den"], ["fim"], out=fim[:], in0=fim[:], in1=den[:], op=ALU.mult)
        freb = fre[:].unsqueeze(2).to_broadcast([128, 16, 16])
        fimb = fim[:].unsqueeze(2).to_broadcast([128, 16, 16])
        dve("tensor_tensor", ["fre", "b_re"], ["bt1"], out=bt1[:], in0=b_re[:], in1=freb, op=ALU.mult)
        dve("tensor_tensor", ["fim", "b_im"], ["bt2"], out=bt2[:], in0=b_im[:], in1=fimb, op=ALU.mult)
        dve("tensor_tensor", ["bt1", "bt2"], ["bb_re"], out=bb_re[:], in0=bt1[:], in1=bt2[:], op=ALU.subtract)
        dve("tensor_tensor", ["fre", "b_im"], ["bt1"], out=bt1[:], in0=b_im[:], in1=freb, op=ALU.mult)
        dve("tensor_tensor", ["fim", "b_re"], ["bt2"], out=bt2[:], in0=b_re[:], in1=fimb, op=ALU.mult)
        dve("tensor_tensor", ["bt1", "bt2"], ["bb_im"], out=bb_im[:], in0=bt1[:], in1=bt2[:], op=ALU.add)
        dve("memset", [], ["CC"], ap=CC[:], constant=0.0)
        for pi in range(16):
            pl = pi % 4
            dve("memset", [], ["bexp"], ap=bexp[:], constant=0.0)
            for g2 in range(2):
                c0 = pl * 32 + g2 * 16
                prt = slice(64 * g2, 64 * g2 + 64)
                dve("tensor_copy", ["bb_re"], ["bexp"], out=bexp[prt, 0, c0:c0 + 16], in_=bb_re[prt, pi, :])
                dve("tensor_copy", ["bb_im"], ["bexp"], out=bexp[prt, 1, c0:c0 + 16], in_=bb_im[prt, pi, :])
                dve("tensor_copy", ["ct_re", "CC"], ["CC"], out=CC[prt, pi, 0, c0:c0 + 16], in_=ct_re[prt, pi, :])
                dve("tensor_scalar", ["ct_im", "CC"], ["CC"], out=CC[prt, pi, 1, c0:c0 + 16], in0=ct_im[prt, pi, :], scalar1=-1.0, scalar2=None, op0=ALU.mult)
            pa, pk = next_pA()
            for ri in range(2):
                P.add("pe", "transpose", r=["bexp", "ident_f"], w=[pk], out=pa[:, ri * 128:(ri + 1) * 128], in_=bexp[:, ri, :], identity=ident_f[:])
            dve("tensor_copy", [pk], ["Lbb"], out=Lbb[:, pi, :, :], in_=pa[:, 0:256].rearrange("p (a b) -> p a b", a=2))
        dve("tensor_copy", [CS], ["tab"], out=Ctab[:, :, 0], in_=cs[:])
        dve("tensor_copy", [SN], ["tab"], out=Stab[:, :, 0], in_=sn[:])
        m = 1
        while m < T2:
            cm = Ctab[:, :, m - 1:m].to_broadcast([128, 16, m])
            smb = Stab[:, :, m - 1:m].to_broadcast([128, 16, m])
            t1 = bt1[:, :, 0:m] if m <= 16 else None
            ta = R0[:, :, 0:m]
            tb = R0[:, :, m:2 * m] if 2 * m <= T2 else None
            dve("tensor_tensor", ["tab"], ["ta"], out=ta, in0=Ctab[:, :, 0:m], in1=cm, op=ALU.mult)
            dve("tensor_tensor", ["tab"], ["tab2"], out=Ctab[:, :, m:2 * m], in0=Stab[:, :, 0:m], in1=smb, op=ALU.mult)
            dve("tensor_tensor", ["ta", "tab2"], ["tab2"], out=Ctab[:, :, m:2 * m], in0=ta, in1=Ctab[:, :, m:2 * m], op=ALU.subtract)
            dve("tensor_tensor", ["tab"], ["ta"], out=ta, in0=Stab[:, :, 0:m], in1=cm, op=ALU.mult)
            dve("tensor_tensor", ["tab"], ["tab3"], out=Stab[:, :, m:2 * m], in0=Ctab[:, :, 0:m], in1=smb, op=ALU.mult)
            dve("tensor_tensor", ["ta", "tab3"], ["tab3"], out=Stab[:, :, m:2 * m], in0=ta, in1=Stab[:, :, m:2 * m], op=ALU.add)
            dve("tensor_copy", ["tab2", "tab3", "tab"], ["tab"], out=tok1[:], in_=tok1[:])
            m *= 2
        dve("tensor_copy", ["mag"], ["rdec"], out=rdec[:], in_=mag[:])
        dve("tensor_copy", ["mag", "ta", "tab"], ["R0"], out=R0[:], in_=mag[:].unsqueeze(2).to_broadcast([128, 16, T2]))
        dve("memset", ["R0"], ["R0"], ap=R0[:, :, 0:1], constant=0.0)
        dve("tensor_tensor", ["mag", "tab"], ["RC"], out=RC[:], in0=mag[:], in1=Ctab[:, :, T2 - 1], op=ALU.mult)
        dve("tensor_tensor", ["mag", "tab"], ["RS"], out=RS[:], in0=mag[:], in1=Stab[:, :, T2 - 1], op=ALU.mult)

        dump_later = [("Ctab", Ctab[:].rearrange("p a b -> p (a b)"), ["tab"]), ("Stab", Stab[:].rearrange("p a b -> p (a b)"), ["tab"]),
                      ("R0", R0[:].rearrange("p a b -> p (a b)"), ["R0"]), ("bb_re", bb_re[:].rearrange("p a b -> p (a b)"), ["bb_re"]),
                      ("Lbb", Lbb[:].rearrange("p a b c -> p (a b c)"), ["Lbb"]), ("CC", CC[:].rearrange("p a b c -> p (a b c)"), ["CC"]),
                      ("fre", fre[:], ["fre"]), ("mag", mag[:], ["mag"]), ("sn", sn[:], ["sncs"]), ("cs", cs[:], ["sncs"])]
        dve("memset", ["Lbb", "CC", "tab", "R0", "RC", "RS", "bb_re", "bb_im", "bexp", "ct_re", "ct_im", "b_re", "b_im", "bt1", "bt2"], ["setup_done"], ap=tok1[:], constant=0.0)
        def wsrc(w_d, k0, nk, c0, ncol):
            return w_d[k0 * 128:(k0 + nk) * 128, c0:c0 + ncol].rearrange("(kc p) n -> p kc n", p=128)

        units = []
        for b in range(NB):
            for u in WIN_ORDER:
                units.append(("win%d" % u, wsrc(w_in_d, 0, 8, u * 512, 512), 8, 512))
            for u in range(2):
                units.append(("ru%d" % u, wsrc(w_ru_d, 0, 8, u * 512, 512), 8, 512))
            for u in range(2):
                units.append(("gv%d" % u, wsrc(w_gv_d, 0, 4, u * 512, 512), 4, 512))
                units.append(("gg%d" % u, wsrc(w_gg_d, 0, 4, u * 512, 512), 4, 512))
            for u in range(2):
                units.append(("wo%d" % u, wsrc(w_out_d, 0, 8, u * 512, 512), 8, 512))
            for u in range(6):
                nco = 512 if u < 5 else 256
                units.append(("fg%d" % u, wsrc(w_fg_d, 0, 8, u * 512, nco), 8, nco))
                units.append(("fu%d" % u, wsrc(w_fu_d, 0, 8, u * 512, nco), 8, nco))
            for cb in range(2):
                for kg in range(3):
                    nk = 8 if kg < 2 else 6
                    units.append(("fd%d_%d" % (cb, kg), wsrc(w_fd_d, kg * 8, nk, cb * 512, 512), nk, 512))
        ust = {"issued": 0, "cur": 0, "done": 0}

        def issue_upto(n):
            n = min(n, len(units), ust["done"] + NRING)
            while ust["issued"] < n:
                i = ust["issued"]
                nm, src, nk, nco = units[i]
                s = i % NRING
                P.add("pool", "dma_start", w=["ring%d" % s], dma="ring%d" % s, out=ring[s][:, 0:nk, 0:nco], in_=src)
                ust["issued"] += 1

        def get_unit(expect):
            i = ust["cur"]
            nm = units[i][0]
            assert nm == expect, (nm, expect)
            issue_upto(i + 1)
            assert ust["issued"] > i, "ring overflow"
            ust["cur"] += 1
            s = i % NRING
            return ring[s], "ring%d" % s

        def release(n=1):
            ust["done"] += n
            issue_upto(ust["done"] + NRING)

        def dump(name, ap, keys):
            if name in dbg_d:
                P.add("pool", "dma_start", r=keys, dma="dbg_" + name, out=dbg_d[name], in_=ap)

        for nm_, ap_, k_ in dump_later:
            dump(nm_, ap_, k_)

        def phase_token(wkeys):
            dve("memset", [], wkeys, ap=tok1[:], constant=0.0)

        def rms_prep(i, src_ap, src_key, gT, gkey):
            dve("memset", [], ["ss0"], ap=ss[:, 0:1], constant=0.0)
            act([src_key], ["ss0"], junk[:], src_ap, AF.Square, accum_out=ss[:, 0:1])
            rsqrt(rinv[:, 0:1], ss[:, 0:1], 1.0 / D, ["ss0"], ["rinv0"])
            act([src_key, "rinv0"], ["xn"], xn[:], src_ap, AF.Copy, scale=rinv[:, 0:1])
            pt, pk = next_pT()
            for kc in range(8):
                P.add("pe", "transpose", r=["xn", "ident_b"], w=[pk], out=pt[:, kc * 128:(kc + 1) * 128], in_=xn[:, kc * 128:(kc + 1) * 128], identity=ident_b[:])
            dve("tensor_tensor", [pk, gkey], ["actT.%d" % i], out=actT[:, :, i * 128:(i + 1) * 128],
                in0=pt[:, :].rearrange("p (a b) -> p a b", a=8), in1=gT[:].unsqueeze(2).to_broadcast([128, 8, 128]), op=ALU.mult)

        ALLACT = ["actT.%d" % i for i in range(4)]
        xtmp = retn

        def stage_A_norm(bn, vt0=None):
            for i in range(4):
                gt = bn * 4 + i
                if vt0 is not None:
                    P.override = [vt0 + i * 9.0, 1e-7]
                    P.pe_delay = 7.0
                P.add("sp", "dma_start", r=(["setup_done"] if bn == 0 else []), w=["retn"], dma="xt", out=xtmp[:], in_=x_d[gt * 128:(gt + 1) * 128, :])
                rms_prep(i, xtmp[:], "retn", gpreT, "gpreT")
                dve("tensor_scalar", ["pos_f", "invf"], ["ang2"], out=ang2[:], in0=invf[:], scalar1=pos_f[:, gt:gt + 1], scalar2=None, op0=ALU.mult)
                sincos(64, ang2[:], ["ang2"], cs_t[:, i, 1, :], cs_t[:, i, 0, :], ["cs.%d" % i])
            P.override = None
            P.pe_delay = 0.0

        for b in range(NB):
            for i in range(4):
                gt = b * 4 + i
                P.add("sp", "dma_start", r=(["setup_done"] if b == 0 else []), w=["x.%d" % i], dma="x%d" % i, out=x_sb[:, i, :], in_=x_d[gt * 128:(gt + 1) * 128, :])
            if b == 0:
                stage_A_norm(0)
            dump("actT", actT[:].rearrange("p a b -> p (a b)"), ALLACT)

            phase_token(["R1F", "R1M"])
            phase_token(["R2S", "R2G"])
            s5vt = [None, 0.0]

            def s5_part1(hc):
                t0 = hc * T2
                for half in range(2):
                    hs = slice(half * 8, half * 8 + 8)
                    wr, wi = (w_re, w_im) if half == 0 else (w_re2, w_im2)
                    wk = "w%d" % half
                    for pgl in range(2):
                        pg = half * 2 + pgl
                        if s5vt[0] is not None:
                            P.override = [s5vt[0] + pg * 0.15 * s5vt[1], 1e-7]
                        pb, pbk = next_pS()
                        pbv = pb[:, :].rearrange("p (a r t) -> p a r t", a=4, r=2)
                        for pl in range(4):
                            pi = pg * 4 + pl
                            for ri in range(2):
                                P.add("pe", "matmul", r=["Lbb", "ussmT"], w=[pbk], out=pbv[:, pl, ri, :], lhsT=Lbb[:, pi, ri, :], rhs=ussmT[:, pg, t0:t0 + T2], start=True, stop=True)
                        a_ = pbv[:, :, 0, :]
                        b_ = pbv[:, :, 1, :]
                        Cc = Ctab[:, pg * 4:pg * 4 + 4, :]
                        Sc = Stab[:, pg * 4:pg * 4 + 4, :]
                        sl = slice(pgl * 4, pgl * 4 + 4)
                        dve("tensor_tensor", [pbk, "tab"], ["s1"], out=s1[:, sl, :], in0=a_, in1=Cc, op=ALU.mult)
                        dve("tensor_tensor", [pbk, "tab"], ["s2"], out=s2[:, sl, :], in0=b_, in1=Sc, op=ALU.mult)
                        dve("tensor_tensor", ["s1", "s2"], [wk + "re"], out=wr[:, sl, :], in0=s1[:, sl, :], in1=s2[:, sl, :], op=ALU.add)
                        dve("tensor_tensor", [pbk, "tab"], ["s1"], out=s1[:, sl, :], in0=b_, in1=Cc, op=ALU.mult)
                        dve("tensor_tensor", [pbk, "tab"], ["s2"], out=s2[:, sl, :], in0=a_, in1=Sc, op=ALU.mult)
                        dve("tensor_tensor", ["s1", "s2"], [wk + "im"], out=wi[:, sl, :], in0=s1[:, sl, :], in1=s2[:, sl, :], op=ALU.subtract)
                    dve("tensor_tensor", [wk + "re", "car%d" % half], [wk + "re"], out=wr[:, :, 0], in0=wr[:, :, 0], in1=car_re[:, hs], op=ALU.add)
                    dve("tensor_tensor", [wk + "im", "car%d" % half], [wk + "im"], out=wi[:, :, 0], in0=wi[:, :, 0], in1=car_im[:, hs], op=ALU.add)
                    R0h = R0[:, hs, :].rearrange("p a b -> p (a b)")
                    dve("tensor_tensor_scan", [wk + "re", "R0"], [wk + "re"], out=wr[:].rearrange("p a b -> p (a b)"), data0=R0h, data1=wr[:].rearrange("p a b -> p (a b)"), initial=0.0, op0=ALU.mult, op1=ALU.add)
                    dve("tensor_tensor_scan", [wk + "im", "R0"], [wk + "im"], out=wi[:].rearrange("p a b -> p (a b)"), data0=R0h, data1=wi[:].rearrange("p a b -> p (a b)"), initial=0.0, op0=ALU.mult, op1=ALU.add)
                    vr, vi = wr[:, :, T2 - 1], wi[:, :, T2 - 1]
                    dve("tensor_tensor", [wk + "re", "RC"], ["cl0"], out=cl[0][:], in0=vr, in1=RC[:, hs], op=ALU.mult)
                    dve("tensor_tensor", [wk + "im", "RS"], ["cl1"], out=cl[1][:], in0=vi, in1=RS[:, hs], op=ALU.mult)
                    dve("tensor_tensor", [wk + "im", "RC"], ["cl2"], out=cl[2][:], in0=vi, in1=RC[:, hs], op=ALU.mult)
                    dve("tensor_tensor", [wk + "re", "RS"], ["cl3"], out=cl[3][:], in0=vr, in1=RS[:, hs], op=ALU.mult)
                    dve("tensor_tensor", ["cl0", "cl1"], ["car%d" % half], out=car_re[:, hs], in0=cl[0][:], in1=cl[1][:], op=ALU.subtract)
                    dve("tensor_tensor", ["cl2", "cl3"], ["car%d" % half], out=car_im[:, hs], in0=cl[2][:], in1=cl[3][:], op=ALU.add)
                    if b == 0 and hc == 0 and half == 0:
                        dump("vre", wr[:].rearrange("p a b -> p (a b)"), [wk + "re"])

            def s5_part2(hc):
                for half in range(2):
                    hs = slice(half * 8, half * 8 + 8)
                    wr, wi = (w_re, w_im) if half == 0 else (w_re2, w_im2)
                    wk = "w%d" % half
                    Ch = Ctab[:, hs, :]
                    Sh = Stab[:, hs, :]
                    pool("tensor_tensor", [wk + "re", "tab"], ["p1"], out=p1[:], in0=wr[:], in1=Ch, op=ALU.mult)
                    pool("tensor_tensor", [wk + "im", "tab"], ["p2"], out=p2[:], in0=wi[:], in1=Sh, op=ALU.mult)
                    pool("tensor_tensor", ["p1", "p2"], ["xr_bf"], out=xr_bf[:, hs, :], in0=p1[:], in1=p2[:], op=ALU.subtract)
                    dve("tensor_tensor", [wk + "im", "tab"], ["s1"], out=s1[:], in0=wi[:], in1=Ch, op=ALU.mult)
                    dve("tensor_tensor", [wk + "re", "tab"], ["s2"], out=s2[:], in0=wr[:], in1=Sh, op=ALU.mult)
                    dve("tensor_tensor", ["s1", "s2"], ["xi_bf"], out=xi_bf[:, hs, :], in0=s1[:], in1=s2[:], op=ALU.add)

            def s5_part3(hc):
                t0 = hc * T2
                py, pyk = next_pS()
                pyv = py[:, 0:4 * T2].rearrange("p (a t) -> p a t", a=4)
                for ta in range(4):
                    for pl in range(4):
                        pi = ta * 4 + pl
                        P.add("pe", "matmul", r=["CC", "xr_bf"], w=[pyk], out=pyv[:, ta, :], lhsT=CC[:, pi, 0, :], rhs=xr_bf[:, pi, :], start=(pl == 0), stop=False)
                        P.add("pe", "matmul", r=["CC", "xi_bf"], w=[pyk], out=pyv[:, ta, :], lhsT=CC[:, pi, 1, :], rhs=xi_bf[:, pi, :], start=False, stop=(pl == 3))
                for ta in range(4):
                    dve("scalar_tensor_tensor", [pyk, "ussmT", "d_T"], ["yt"], out=yt[:, ta, :], in0=ussmT[:, ta, t0:t0 + T2], scalar=d_T[:, ta:ta + 1], in1=pyv[:, ta, :], op0=ALU.mult, op1=ALU.add)
                if b == 0 and hc == 0:
                    dump("yt", yt[:].rearrange("p a b -> p (a b)"), ["yt"])
                pool("tensor_tensor", ["yt"], ["g1"], out=gl1[:], in0=yt[:], in1=yt[:], op=ALU.mult)
                pool("tensor_scalar", ["g1"], ["g1"], out=gl1[:], in0=gl1[:], scalar1=0.044715, scalar2=1.0, op0=ALU.mult, op1=ALU.add)
                pool("tensor_tensor", ["g1", "yt"], ["g1"], out=gl1[:], in0=gl1[:], in1=yt[:], op=ALU.mult)
                act(["g1"], ["g2"], gl2[:], gl1[:], AF.Sigmoid, scale=1.5957691216057308)
                pool("tensor_tensor", ["g2", "yt"], ["zT"], out=zT[:, :, t0:t0 + T2], in0=gl2[:], in1=yt[:], op=ALU.mult)

            for idx, u in enumerate(WIN_ORDER):
                rg, rk = get_unit("win%d" % u)
                if u == 6:
                    for ta in range(4):
                        pa, pk = next_pA()
                        for kc in range(8):
                            P.add("pe", "matmul", r=[rk] + ALLACT, w=[pk], out=pa[:, :], lhsT=rg[:, kc, ta * 128:(ta + 1) * 128], rhs=actT[:, kc, :], start=(kc == 0), stop=(kc == 7))
                        act([pk], ["ussmT"], ussmT[:, ta, :], pa[:, :], AF.Copy)
                    release()
                    vt_s5_start = P.clock
                    continue
                for i in range(4):
                    pa, pk = next_pA()
                    for kc in range(8):
                        P.add("pe", "matmul", r=[rk, "actT.%d" % i], w=[pk], out=pa[:, :], lhsT=actT[:, kc, i * 128:(i + 1) * 128], rhs=rg[:, kc, :], start=(kc == 0), stop=(kc == 7))
                    if u < 2:
                        act([pk], ["qkraw"], qkraw[:], pa[:, :], AF.Copy)
                        xv = qkraw[:, :].rearrange("p (h t f) -> p h t f", h=4, t=2)
                        x1, x2 = xv[:, :, 0, :], xv[:, :, 1, :]
                        cb_ = cs_t[:, i, 0:1, :].to_broadcast([128, 4, 64])
                        sb_ = cs_t[:, i, 1:2, :].to_broadcast([128, 4, 64])
                        ck = "cs.%d" % i
                        pool("tensor_tensor", ["qkraw", ck], ["rt0"], out=rt[0][:], in0=x1, in1=cb_, op=ALU.mult)
                        pool("tensor_tensor", ["qkraw", ck], ["rt1"], out=rt[1][:], in0=x2, in1=sb_, op=ALU.mult)
                        pool("tensor_tensor", ["qkraw", ck], ["rt2"], out=rt[2][:], in0=x1, in1=sb_, op=ALU.mult)
                        pool("tensor_tensor", ["qkraw", ck], ["rt3"], out=rt[3][:], in0=x2, in1=cb_, op=ALU.mult)
                        pool("tensor_tensor", ["rt0", "rt1"], ["ro"], out=ro[:, :, 0, :], in0=rt[0][:], in1=rt[1][:], op=ALU.subtract)
                        pool("tensor_tensor", ["rt2", "rt3", "ro"], ["ro"], out=ro[:, :, 1, :], in0=rt[2][:], in1=rt[3][:], op=ALU.add)
                        scl = (qscale if u == 0 else kscale)[:].unsqueeze(2).to_broadcast([128, 4, 128])
                        pool("tensor_tensor", ["ro", "qscale", "kscale", "R1M"], ["qk.%d" % i],
                             out=qk_sb[:, i, u * 512:(u + 1) * 512].rearrange("p (h f) -> p h f", h=4),
                             in0=ro[:].rearrange("p h t f -> p h (t f)"), in1=scl, op=ALU.mult)
                        if b == 0 and i == 0:
                            pool("tensor_tensor", ["ro", "qscale", "kscale"], ["qk32", "retn"],
                                 out=qk32[:, u * 512:(u + 1) * 512].rearrange("p (h f) -> p h f", h=4),
                                 in0=ro[:].rearrange("p h t f -> p h (t f)"), in1=scl, op=ALU.mult)
                    elif u < 4:
                        act([pk, "R1M"], ["v.%d" % i], v_sb[:, i, (u - 2) * 512:(u - 1) * 512], pa[:, :], AF.Copy)
                    elif u < 6:
                        act([pk, "R1M"], ["sg.%d" % i], sg_sb[:, i, (u - 4) * 512:(u - 3) * 512], pa[:, :], AF.Silu)
                    else:
                        act([pk, "R2G"], ["gates.%d" % i], gates[:, i, (u - 7) * 512:(u - 6) * 512], pa[:, :], AF.Sigmoid)
                release()
            dump("qk", qk_sb, ["qk.%d" % i for i in range(4)])
            dump("v", v_sb, ["v.%d" % i for i in range(4)])
            dump("sg", sg_sb, ["sg.%d" % i for i in range(4)])
            dump("ussmT", ussmT[:].rearrange("p a b -> p (a b)"), ["ussmT"])

            for i in range(4):
                pt, ptk = next_pT()
                for j in range(8):
                    P.add("pe", "transpose", r=["qk.%d" % i, "ident_b", "R1M"], w=[ptk], out=pt[:, j * 128:(j + 1) * 128], in_=qk_sb[:, i, j * 128:(j + 1) * 128], identity=ident_b[:])
                act([ptk], ["qkT"], qkT[:].rearrange("p a b -> p (a b)"), pt[:, :], AF.Copy)
                if b == 0 and i == 0:
                    for hf in range(2):
                        pa32, pk32 = next_pA()
                        for jj in range(4):
                            j = hf * 4 + jj
                            P.add("pe", "transpose", r=["qk32", "retn", "ident_f"], w=[pk32], out=pa32[:, jj * 128:(jj + 1) * 128], in_=qk32[:, j * 128:(j + 1) * 128], identity=ident_f[:])
                        act([pk32], ["qkT32", "tt", "sgt"], qkT32[:, hf * 4:hf * 4 + 4, :], pa32[:, :].rearrange("p (a b) -> p a b", a=4), AF.Copy)
                psc, psk = next_pA()
                for h in range(NH):
                    if b == 0 and i == 0:
                        P.add("pe", "matmul", r=["qkT32", "tt", "sgt"], w=[psk], out=psc[:, h * 128:(h + 1) * 128], lhsT=qkT32[:, 4 + h, :], rhs=qkT32[:, h, :], start=True, stop=True)
                    else:
                        P.add("pe", "matmul", r=["qkT"], w=[psk], out=psc[:, h * 128:(h + 1) * 128], lhsT=qkT[:, 4 + h, :], rhs=qkT[:, h, :], start=True, stop=True)
                dve("tensor_tensor", [psk, "maskT"], ["sc_bf"], out=sc_bf[:], in0=psc[:, :].rearrange("p (h i) -> p h i", h=4),
                    in1=maskT[:].unsqueeze(1).to_broadcast([128, 4, 128]), op=ALU.mult)
                pr = []
                for hp in range(2):
                    pa, pk = next_pA()
                    pr.append((pa, pk))
                    for hh in range(2):
                        h = hp * 2 + hh
                        P.add("pe", "matmul", r=["sc_bf", "v.%d" % i, "R1M"], w=[pk], out=pa[:, hh * 256:(hh + 1) * 256], lhsT=sc_bf[:, h, :], rhs=v_sb[:, i, h * 256:(h + 1) * 256], start=True, stop=False)
                        P.add("pe", "matmul", r=["qkT", "R_bf.%d" % h], w=[pk], out=pa[:, hh * 256:(hh + 1) * 256], lhsT=qkT[:, h, :], rhs=R_bf[:, h, :], start=False, stop=True)
                for hp in range(2):
                    pa, pk = next_pA()
                    for hh in range(2):
                        h = hp * 2 + hh
                        P.add("pe", "matmul", r=["qk.%d" % i, "v.%d" % i, "R1M"], w=[pk], out=pa[:, hh * 256:(hh + 1) * 256], lhsT=qk_sb[:, i, 512 + h * 128:512 + (h + 1) * 128], rhs=v_sb[:, i, h * 256:(h + 1) * 256], start=True, stop=True)
                    for hh in range(2):
                        h = hp * 2 + hh
                        dve("scalar_tensor_tensor", [pk, "Rs.%d" % h], ["Rs.%d" % h], out=Rs[:, h, :], in0=Rs[:, h, :], scalar=GAM128[h], in1=pa[:, hh * 256:(hh + 1) * 256], op0=ALU.mult, op1=ALU.add)
                        act(["Rs.%d" % h], ["R_bf.%d" % h], R_bf[:, h, :], Rs[:, h, :], AF.Copy, scale=GAM128[h])
                for hp in range(2):
                    pa, pk = pr[hp]
                    for hh in range(2):
                        dve("bn_stats", [pk], ["st6"], out=st6[:, hp * 2 + hh, :], in_=pa[:, hh * 256:(hh + 1) * 256])
                for h in range(NH):
                    dve("bn_aggr", ["st6"], ["mv"], out=mv[:, h, :], in_=st6[:, h, :])
                rsqrt(rstd[:], mv[:, :, 1], 1.0, ["mv"], ["rstd"])
                for hp in range(2):
                    pa, pk = pr[hp]
                    for hh in range(2):
                        h = hp * 2 + hh
                        dve("tensor_scalar", [pk, "mv", "rstd"], ["retn"], out=retn[:, h * 256:(h + 1) * 256], in0=pa[:, hh * 256:(hh + 1) * 256],
                            scalar1=mv[:, h, 0:1], scalar2=rstd[:, h:h + 1], op0=ALU.subtract, op1=ALU.mult)
                if b == 0 and i == 0:
                    dump("retn", retn[:], ["retn"])
                if b == 0:
                    dump("retn%d" % i, retn[:], ["retn"])
                dve("tensor_tensor", ["retn", "sg.%d" % i, "R1M"], ["gated"], out=gated[:], in0=retn[:], in1=sg_sb[:, i, :], op=ALU.mult)
                pt, ptk = next_pT()
                for kc in range(8):
                    P.add("pe", "transpose", r=["gated", "ident_b"], w=[ptk], out=pt[:, kc * 128:(kc + 1) * 128], in_=gated[:, kc * 128:(kc + 1) * 128], identity=ident_b[:])
                act([ptk] + ["actT.%d" % i], ["actT.%d" % i], actT[:, :, i * 128:(i + 1) * 128], pt[:, :].rearrange("p (a b) -> p a b", a=8), AF.Copy)

            dump("zT", zT[:].rearrange("p a b -> p (a b)"), ["zT"])

            for u in range(2):
                rg, rk = get_unit("ru%d" % u)
                for i in range(4):
                    pa, pk = next_pA()
                    for kc in range(8):
                        P.add("pe", "matmul", r=[rk, "actT.%d" % i], w=[pk], out=pa[:, :], lhsT=actT[:, kc, i * 128:(i + 1) * 128], rhs=rg[:, kc, :], start=(kc == 0), stop=(kc == 7))
                    dve("tensor_tensor", [pk, "gates.%d" % i, "R1M", "R2G"], ["mix.%d.%d" % (i, u), "sg.%d" % i], out=mix_sb[:, i, u * 512:(u + 1) * 512], in0=pa[:, :], in1=gates[:, i, u * 512:(u + 1) * 512], op=ALU.mult)
                release()
            vt_s5_end = P.clock
            span = (vt_s5_end - vt_s5_start) * 0.92 / 8.0
            for f in range(8):
                base = vt_s5_start + f * span
                s5vt[0], s5vt[1] = base, span
                s5_part1(f)
                s5_part2(f)
                P.override = [base + S5_P3_DELAY * span, 1e-7]
                s5_part3(f)
            P.override = None
            dump("mixa", mix_sb, ["mix.%d.%d" % (i, u) for i in range(4) for u in range(2)])
            for u in range(2):
                rgv, rkv = get_unit("gv%d" % u)
                rgg, rkg = get_unit("gg%d" % u)
                for i in range(4):
                    pv, pvk = next_pA()
                    pg_, pgk = next_pA()
                    for kc in range(4):
                        P.add("pe", "matmul", r=[rkv, "zT"], w=[pvk], out=pv[:, :], lhsT=zT[:, kc, i * 128:(i + 1) * 128], rhs=rgv[:, kc, :], start=(kc == 0), stop=(kc == 3))
                    for kc in range(4):
                        P.add("pe", "matmul", r=[rkg, "zT"], w=[pgk], out=pg_[:, :], lhsT=zT[:, kc, i * 128:(i + 1) * 128], rhs=rgg[:, kc, :], start=(kc == 0), stop=(kc == 3))
                    act([pgk], ["sgt"], sgt[:], pg_[:, :], AF.Sigmoid)
                    dve("tensor_tensor", [pvk, "sgt"], ["tt"], out=tt[:], in0=pv[:, :], in1=sgt[:], op=ALU.mult)
                    dve("tensor_tensor", ["tt", "gates.%d" % i, "R2G"], ["tt"], out=tt[:], in0=tt[:], in1=gates[:, i, 1024 + u * 512:1024 + (u + 1) * 512], op=ALU.mult)
                    mk = "mix.%d.%d" % (i, u)
                    dve("tensor_tensor", ["tt", mk, "R1M"], [mk], out=mix_sb[:, i, u * 512:(u + 1) * 512], in0=tt[:], in1=mix_sb[:, i, u * 512:(u + 1) * 512], op=ALU.add)
                release(2)
            for i in range(4):
                pt, ptk = next_pT()
                for kc in range(8):
                    P.add("pe", "transpose", r=["mix.%d.0" % i, "mix.%d.1" % i, "ident_b", "R1M"], w=[ptk], out=pt[:, kc * 128:(kc + 1) * 128], in_=mix_sb[:, i, kc * 128:(kc + 1) * 128], identity=ident_b[:])
                act([ptk], ["actT.%d" % i], actT[:, :, i * 128:(i + 1) * 128], pt[:, :].rearrange("p (a b) -> p a b", a=8), AF.Copy)

            dump("mix", mix_sb, ["mix.%d.%d" % (i, u) for i in range(4) for u in range(2)])
            dump("gates", gates, ["gates.%d" % i for i in range(4)])
            phase_token(["R2G", "R2S"] + ["gates.%d" % i for i in range(4)])
            for u in range(2):
                rg, rk = get_unit("wo%d" % u)
                for i in range(4):
                    pa, pk = next_pA()
                    for kc in range(8):
                        P.add("pe", "matmul", r=[rk, "actT.%d" % i], w=[pk], out=pa[:, :], lhsT=actT[:, kc, i * 128:(i + 1) * 128], rhs=rg[:, kc, :], start=(kc == 0), stop=(kc == 7))
                    act([pk, "R2S"], ["stg.%d.%d" % (i, u)], stg[:, i, u * 512:(u + 1) * 512], pa[:, :], AF.Copy)
                    dve("memset", [], ["ssq.%d.%d" % (i, u)], ap=ss2[:, 2 * i + u:2 * i + u + 1], constant=0.0)
                    act(["stg.%d.%d" % (i, u), "R2S"], ["ssq.%d.%d" % (i, u)], junk[:, 0:512], stg[:, i, u * 512:(u + 1) * 512], AF.Square, accum_out=ss2[:, 2 * i + u:2 * i + u + 1])
                release()

            def post_norm_res(i, g_rep, gkey, dst_is_x):
                k0, k1 = "ssq.%d.0" % i, "ssq.%d.1" % i
                dve("tensor_tensor", [k0, k1], ["rinv1"], out=rinv[:, 1 + i:2 + i], in0=ss2[:, 2 * i:2 * i + 1], in1=ss2[:, 2 * i + 1:2 * i + 2], op=ALU.add)
                rsqrt(rinv[:, 1 + i:2 + i], rinv[:, 1 + i:2 + i], 1.0 / D, ["rinv1"], ["rinv1"])
                sk = ["stg.%d.0" % i, "stg.%d.1" % i]
                dve("scalar_tensor_tensor", sk + ["rinv1", gkey, "R2S"], sk, out=stg[:, i, :], in0=stg[:, i, :], scalar=rinv[:, 1 + i:2 + i], in1=g_rep[:], op0=ALU.mult, op1=ALU.mult)
                if dst_is_x:
                    pool("tensor_tensor", sk + ["x.%d" % i, "R2S"], ["x.%d" % i], out=x_sb[:, i, :], in0=stg[:, i, :], in1=x_sb[:, i, :], op=ALU.add)
                else:
                    pool("tensor_tensor", sk + ["x.%d" % i, "R2S"], sk, out=stg[:, i, :], in0=stg[:, i, :], in1=x_sb[:, i, :], op=ALU.add)

            for i in range(4):
                post_norm_res(i, gpost, "gpost", True)
            if b == 0:
                dump("h", x_sb[:, 0, :], ["x.0"])

            for i in range(4):
                rms_prep(i, x_sb[:, i, :], "x.%d" % i, gffnT, "gffnT")
            phase_token(["R1M", "R1F"] + ["qk.%d" % i for i in range(4)] + ["v.%d" % i for i in range(4)] + ["sg.%d" % i for i in range(4)]
                        + ["mix.%d.%d" % (i, u) for i in range(4) for u in range(2)])
            for u in range(6):
                rgg, rkg = get_unit("fg%d" % u)
                rgu, rku = get_unit("fu%d" % u)
                nft = 4 if u < 5 else 2
                for ft in range(nft):
                    pg_, pgk = next_pA()
                    pu, puk = next_pA()
                    for kc in range(8):
                        P.add("pe", "matmul", r=[rkg] + ALLACT, w=[pgk], out=pg_[:, :], lhsT=rgg[:, kc, ft * 128:(ft + 1) * 128], rhs=actT[:, kc, :], start=(kc == 0), stop=(kc == 7))
                    for kc in range(8):
                        P.add("pe", "matmul", r=[rku] + ALLACT, w=[puk], out=pu[:, :], lhsT=rgu[:, kc, ft * 128:(ft + 1) * 128], rhs=actT[:, kc, :], start=(kc == 0), stop=(kc == 7))
                    act([pgk], ["sgt"], sgt[:], pg_[:, :], AF.Silu)
                    dve("tensor_tensor", [puk, "sgt", "R1F"], ["hid.%d" % (u * 4 + ft)], out=hidT[:, u * 4 + ft, :], in0=pu[:, :], in1=sgt[:], op=ALU.mult)
                release(2)

            vt_I_start = P.clock
            for cb in range(2):
                rgs = [get_unit("fd%d_%d" % (cb, kg)) for kg in range(3)]
                for i in range(4):
                    pa, pk = next_pA()
                    for kc in range(22):
                        rg, rk = rgs[kc // 8]
                        P.add("pe", "matmul", r=[rk, "hid.%d" % kc, "R1F"], w=[pk], out=pa[:, :], lhsT=hidT[:, kc, i * 128:(i + 1) * 128], rhs=rg[:, kc % 8, :], start=(kc == 0), stop=(kc == 21))
                    act([pk, "R2S"], ["stg.%d.%d" % (i, cb)], stg[:, i, cb * 512:(cb + 1) * 512], pa[:, :], AF.Copy)
                    dve("memset", [], ["ssq.%d.%d" % (i, cb)], ap=ss2[:, 2 * i + cb:2 * i + cb + 1], constant=0.0)
                    act(["stg.%d.%d" % (i, cb), "R2S"], ["ssq.%d.%d" % (i, cb)], junk[:, 0:512], stg[:, i, cb * 512:(cb + 1) * 512], AF.Square, accum_out=ss2[:, 2 * i + cb:2 * i + cb + 1])
                release(3)
            for i in range(4):
                gt = b * 4 + i
                post_norm_res(i, gfpost, "gfpost", False)
                P.add("sp", "dma_start", r=["stg.%d.0" % i, "stg.%d.1" % i], w=["yout.%d" % i], dma="y%d" % i, out=y_d[gt * 128:(gt + 1) * 128, :], in_=stg[:, i, :])
                P.ops[-1].r = P.ops[-1].r + ("R2S",)
            if b + 1 < NB:
                stage_A_norm(b + 1, vt_I_start - 3.0)

        P.finalize()
    return nc


def _consts():
    idx = np.arange(128, dtype=np.float64)
    qs = np.stack([np.array(GAM[h], np.float64) ** (idx + 1.0) for h in range(NH)], axis=1)
    ks = np.stack([np.array(GAM[h], np.float64) ** (-(idx + 1.0)) * (128.0 ** -0.5) for h in range(NH)], axis=1)
    maskT = (idx[:, None] <= idx[None, :]).astype(np.float32)
    invf = (10000.0 ** (-np.arange(0, 128, 2, dtype=np.float32) / np.float32(128))).astype(np.float32)
    return dict(
        ident_f=np.eye(128, dtype=np.float32),
        maskT=maskT,
        qscale=qs.astype(np.float32),
        kscale=ks.astype(np.float32),
        invfreq=np.ascontiguousarray(np.broadcast_to(invf[None, :], (128, 64))).astype(np.float32),
    )


def _shared_inputs(inp):
    f = lambda a: np.ascontiguousarray(np.asarray(a, dtype=np.float32))
    pair = lambda a: f(np.asarray(a).reshape(16, 2, 64).transpose(1, 2, 0).reshape(128, 16))
    m = dict(
        w_in=f(inp["w_in"][0]), w_ret_up=f(inp["w_ret_up"][0]), w_glu_val=f(inp["w_glu_val"][0]),
        w_glu_gate=f(inp["w_glu_gate"][0]), w_out=f(inp["w_out"][0]), w_ffn_gate=f(inp["w_ffn_gate"][0]),
        w_ffn_up=f(inp["w_ffn_up"][0]), w_ffn_down=f(inp["w_ffn_down"][0]),
        g_pre_T=f(np.asarray(inp["mix_pre_norm"][0]).reshape(8, 128).T),
        g_ffn_T=f(np.asarray(inp["ffn_pre_norm"][0]).reshape(8, 128).T),
        g_post=f(np.asarray(inp["mix_post_norm"][0]).reshape(1, D)),
        g_fpost=f(np.asarray(inp["ffn_post_norm"][0]).reshape(1, D)),
        a_re=pair(inp["ssm_a_re"][0]), a_im=pair(inp["ssm_a_im"][0]),
        log_dt=f(np.broadcast_to(np.asarray(inp["ssm_log_dt"][0]).reshape(16, 2, 1), (16, 2, 64)).transpose(1, 2, 0).reshape(128, 16)),
        b_re=f(np.asarray(inp["ssm_b_re"][0]).reshape(16, 2, 64, 16).transpose(1, 2, 0, 3).reshape(128, 16, 16)),
        b_im=f(np.asarray(inp["ssm_b_im"][0]).reshape(16, 2, 64, 16).transpose(1, 2, 0, 3).reshape(128, 16, 16)),
        ct_re=f(np.asarray(inp["ssm_c_re"][0]).reshape(16, 2, 16, 64).transpose(1, 3, 0, 2).reshape(128, 16, 16)),
        ct_im=f(np.asarray(inp["ssm_c_im"][0]).reshape(16, 2, 16, 64).transpose(1, 3, 0, 2).reshape(128, 16, 16)),
        d_T=f(np.asarray(inp["ssm_d"][0]).reshape(4, 128).T),
    )
    m.update(_consts())
    return m


def make_in_maps(inp, ntok, cores):
    shared = _shared_inputs(inp)
    maps = []
    x = np.asarray(inp["x"])
    pos = np.asarray(inp["positions"])
    for c in cores:
        mm = dict(shared)
        mm["x"] = np.ascontiguousarray(x[c, :ntok, :], dtype=np.float32)
        mm["pos"] = np.ascontiguousarray(pos[c, :ntok].reshape(ntok // 128, 128).T).astype(np.int32)
        maps.append(mm)
    return maps


def kernel(**inputs):
    nc = build(SEQ)
    cores = list(range(8))
    in_maps = make_in_maps(inputs, SEQ, cores)
    res = run_bass_kernel_spmd(nc, in_maps, core_ids=cores)
    out = np.stack([np.asarray(r["y"]) for r in res.results], axis=0)
    return out.astype(np.float32)
```

```python
import math
from contextlib import ExitStack

import numpy as np
import concourse.bass as bass
import concourse.mybir as mybir
from concourse.bass_utils import run_bass_kernel_spmd

F32 = mybir.dt.float32
BF16 = mybir.dt.bfloat16
I32 = mybir.dt.int32
AF = mybir.ActivationFunctionType
ALU = mybir.AluOpType
AX = mybir.AxisListType

D = 1024
SEQ = 8192
NH = 4
INW = 5632
FFN = 2816
EPS = 1e-6
T2 = 64
TWO_PI = 2.0 * math.pi
C1 = 6.28125
C2 = TWO_PI - C1
MAGIC = 12582912.0
GAM = [1.0 - 2.0 ** (-5.0 - h) for h in range(NH)]
GAM128 = [g ** 128 for g in GAM]
SAME_DIST = 4
S5_P3_DELAY = 1.2
WIN_ORDER = [6, 0, 1, 2, 3, 4, 5, 7, 8, 9, 10]


class _Op:
    __slots__ = ("eng", "meth", "kw", "r", "w", "dma", "deps", "need", "sem", "val", "raw", "pos", "vt")


class Prog:
    def __init__(self, nc, es):
        self.nc = nc
        self.es = es
        self.ops = []
        self.engs = {"pe": nc.tensor, "act": nc.scalar, "dve": nc.vector, "pool": nc.gpsimd, "sp": nc.sync}
        self.clock = 0.0
        self.override = None
        self.pe_delay = 0.0

    def add(self, eng, meth, r=(), w=(), dma=None, **kw):
        o = _Op()
        o.eng, o.meth, o.kw, o.r, o.w, o.dma = eng, meth, kw, tuple(r), tuple(w), dma
        if self.override is not None:
            o.vt = self.override[0] + (self.pe_delay if eng == "pe" else 0.0)
            self.override[0] += self.override[1]
        else:
            if eng == "pe":
                if meth == "matmul":
                    n = 1
                    for d_ in kw["rhs"].shape[1:]:
                        n *= d_
                    self.clock += n / 2400.0 + 0.02
                else:
                    self.clock += 0.07
            o.vt = self.clock
        self.ops.append(o)
        return o

    def finalize(self):
        import heapq
        nc = self.nc
        ops = self.ops
        n = len(ops)
        last_w = {}
        readers = {}
        for i, o in enumerate(ops):
            deps = set()
            for k in o.r:
                if k in last_w:
                    deps.add(last_w[k])
            for k in o.w:
                if k in last_w:
                    deps.add(last_w[k])
                deps.update(readers.get(k, ()))
            deps.discard(i)
            o.deps = deps
            o.raw = set(last_w[k] for k in o.r if k in last_w)
            for k in o.r:
                readers.setdefault(k, []).append(i)
            for k in o.w:
                last_w[k] = i
                readers[k] = []
            o.need = False
        succ = [[] for _ in range(n)]
        indeg = [0] * n
        for i, o in enumerate(ops):
            indeg[i] = len(o.deps)
            for d in o.deps:
                succ[d].append(i)
        heap = [(ops[i].vt, i) for i in range(n) if indeg[i] == 0]
        heapq.heapify(heap)
        order = []
        while heap:
            _, i = heapq.heappop(heap)
            order.append(i)
            for j in succ[i]:
                indeg[j] -= 1
                if indeg[j] == 0:
                    heapq.heappush(heap, (max(ops[j].vt, ops[i].vt), j))
        assert len(order) == n
        epos = {}
        rank = [0] * n
        for r_, i in enumerate(order):
            o = ops[i]
            rank[i] = r_
            o.pos = epos.get(o.eng, 0)
            epos[o.eng] = o.pos + 1
        for i in order:
            o = ops[i]
            best = {}
            for d in o.deps:
                p = ops[d]
                if p.dma is None and p.eng == o.eng:
                    if not (o.eng in ("dve", "act", "pool") and o.dma is None and d in o.raw and (o.pos - p.pos) <= SAME_DIST):
                        continue
                k = ("d", p.dma) if p.dma is not None else ("e", p.eng)
                if k not in best or rank[d] > rank[best[k]]:
                    best[k] = d
            o.deps = set(best.values())
            for d in o.deps:
                ops[d].need = True
        sems = {}

        def getsem(name):
            if name not in sems:
                sems[name] = self.es.enter_context(nc.semaphore(name.replace(".", "_")))
            return sems[name]

        cnt = {}
        for i in order:
            o = ops[i]
            if o.dma is not None:
                nm = "d_" + o.dma
                cnt[nm] = cnt.get(nm, 0) + 16
                o.sem, o.val = nm, cnt[nm]
            elif o.need:
                nm = "e_" + o.eng
                cnt[nm] = cnt.get(nm, 0) + 1
                o.sem, o.val = nm, cnt[nm]
            else:
                o.sem, o.val = None, 0
        waited = {e: {} for e in self.engs}
        for i in order:
            o = ops[i]
            e = self.engs[o.eng]
            need = {}
            for d in o.deps:
                p = ops[d]
                if p.val > need.get(p.sem, 0):
                    need[p.sem] = p.val
            for s_, v in need.items():
                if waited[o.eng].get(s_, 0) < v:
                    e.wait_ge(getsem(s_), v)
                    waited[o.eng][s_] = v
            ins = getattr(e, o.meth)(**o.kw)
            if o.sem is not None and (o.dma is not None or o.need):
                ins.then_inc(getsem(o.sem), 16 if o.dma is not None else 1)
        for s_, v in cnt.items():
            if s_.startswith("d_"):
                nc.sync.wait_ge(getsem(s_), v)


def build(ntok, dbg=()):
    assert ntok % 512 == 0
    NB = ntok // 512
    NT = ntok // 128
    nc = bass.Bass("TRN2", target_bir_lowering=False)

    def din(name, shape, dt=F32):
        return nc.dram_tensor(name, list(shape), dt, kind="ExternalInput").ap()

    x_d = din("x", [ntok, D])
    pos_d = din("pos", [128, NT], I32)
    w_in_d = din("w_in", [D, INW])
    w_ru_d = din("w_ret_up", [D, D])
    w_gv_d = din("w_glu_val", [512, D])
    w_gg_d = din("w_glu_gate", [512, D])
    w_out_d = din("w_out", [D, D])
    w_fg_d = din("w_ffn_gate", [D, FFN])
    w_fu_d = din("w_ffn_up", [D, FFN])
    w_fd_d = din("w_ffn_down", [FFN, D])
    gpreT_d = din("g_pre_T", [128, 8])
    gffnT_d = din("g_ffn_T", [128, 8])
    gpost_d = din("g_post", [1, D])
    gfpost_d = din("g_fpost", [1, D])
    are_d = din("a_re", [128, 16])
    aim_d = din("a_im", [128, 16])
    ldt_d = din("log_dt", [128, 16])
    bre_d = din("b_re", [128, 16, 16])
    bim_d = din("b_im", [128, 16, 16])
    cre_d = din("ct_re", [128, 16, 16])
    cim_d = din("ct_im", [128, 16, 16])
    dT_d = din("d_T", [128, 4])
    identf_d = din("ident_f", [128, 128])
    maskT_d = din("maskT", [128, 128])
    qs_d = din("qscale", [128, 4])
    ks_d = din("kscale", [128, 4])
    invf_d = din("invfreq", [128, 64])
    y_d = nc.dram_tensor("y", [ntok, D], F32, kind="ExternalOutput").ap()
    dbg_d = {}
    for nm, shape in dbg:
        dbg_d[nm] = nc.dram_tensor("dbg_" + nm, list(shape), F32, kind="ExternalOutput").ap()

    es = ExitStack()
    with es:
        def sb(name, shape, dt=F32):
            return es.enter_context(nc.sbuf_tensor("s_" + name, list(shape), dt))

        def ps(name, shape, dt=F32):
            return es.enter_context(nc.psum_tensor("p_" + name, list(shape), dt))

        ident_f = sb("ident_f", [128, 128])
        ident_b = sb("ident_b", [128, 128], BF16)
        maskT = sb("maskT", [128, 128])
        qscale = sb("qscale", [128, 4])
        kscale = sb("kscale", [128, 4])
        invf = sb("invf", [128, 64])
        gpreT = sb("gpreT", [128, 8])
        gffnT = sb("gffnT", [128, 8])
        gpost = sb("gpost", [128, D])
        gfpost = sb("gfpost", [128, D])
        pos_i = sb("pos_i", [128, NT], I32)
        pos_f = sb("pos_f", [128, NT])
        a_re = sb("a_re", [128, 16]); a_im = sb("a_im", [128, 16]); ldt = sb("ldt", [128, 16])
        d_T = sb("d_T", [128, 4])
        Lbb = sb("Lbb", [128, 16, 2, 128], BF16)
        CC = sb("CC", [128, 16, 2, 128], BF16)
        Ctab = sb("Ctab", [128, 16, T2]); Stab = sb("Stab", [128, 16, T2]); R0 = sb("R0", [128, 16, T2])
        rdec = sb("rdec", [128, 16])
        car_re = sb("car_re", [128, 16]); car_im = sb("car_im", [128, 16])
        sm = [sb("sm%d" % i, [128, 16]) for i in range(12)]
        Rs = sb("Rs", [128, NH, 256]); R_bf = sb("R_bf", [128, NH, 256], BF16)
        x_sb = sb("x_sb", [128, 4, D])
        xflat = x_sb[:].rearrange("p a b -> p (a b)")
        _v = lambda k: xflat[:, k * 256:(k + 1) * 256].rearrange("p (a b) -> p a b", a=16)
        b_re, b_im, ct_re, ct_im, bb_re, bb_im, bt1, bt2 = [_v(k) for k in range(8)]
        bexp = xflat[:, 2048:2304].rearrange("p (a b) -> p a b", a=2)
        actT = sb("actT", [128, 8, 512], BF16)
        uT = sb("uT", [128, 8, 512], BF16)
        reg1 = sb("reg1", [128, 12288], BF16)
        qk_sb = reg1[:, 0:4096].rearrange("p (a b) -> p a b", a=4)
        v_sb = reg1[:, 4096:8192].rearrange("p (a b) -> p a b", a=4)
        sg_sb = reg1[:, 8192:12288].rearrange("p (a b) -> p a b", a=4)
        mix_sb = sg_sb
        hidT = reg1[:, 0:11264].rearrange("p (a b) -> p a b", a=22)
        reg2 = sb("reg2", [128, 8192], BF16)
        gates = reg2[:, :].rearrange("p (a b) -> p a b", a=4)
        stg = reg2[:, :].bitcast(F32).rearrange("p (a b) -> p a b", a=4)
        ussmT = sb("ussmT", [128, 4, 512], BF16)
        zT = sb("zT", [128, 4, 512], BF16)
        xn = sb("xn", [128, D], BF16)
        ss = sb("ss", [128, 8])
        rinv = sb("rinv", [128, 8])
        ss2 = sb("ss2", [128, 8])
        halfpi = sb("halfpi", [128, 1])
        epst = sb("epst", [128, 1])
        cs_t = sb("cs_t", [128, 4, 2, 64])
        ang2 = sb("ang2", [128, 64]); ang = sb("ang", [128, 64]); kf = sb("kf", [128, 64]); rr = sb("rr", [128, 64]); ab = sb("ab", [128, 64])
        rt = [sb("rt%d" % i, [128, 4, 64]) for i in range(4)]
        ro = sb("ro", [128, 4, 2, 64])
        qkT = sb("qkT", [128, 8, 128], BF16)
        sc_bf = sb("sc_bf", [128, NH, 128], BF16)
        retn = sb("retn", [128, D])
        gated = sb("gated", [128, D], BF16)
        junk = gated
        st6 = sb("st6", [128, NH, 6]); mv = sb("mv", [128, NH, 2]); rstd = sb("rstd", [128, NH])
        w_re = sb("w_re", [128, 8, T2]); w_im = sb("w_im", [128, 8, T2])
        w_re2 = sb("w_re2", [128, 8, T2]); w_im2 = sb("w_im2", [128, 8, T2])
        s1 = sb("s1", [128, 8, T2]); s2 = sb("s2", [128, 8, T2])
        xr_bf = sb("xr_bf", [128, 16, T2], BF16); xi_bf = sb("xi_bf", [128, 16, T2], BF16)
        cl = [sb("cl%d" % i, [128, 8]) for i in range(4)]
        p1 = sb("p1", [128, 8, T2]); p2 = sb("p2", [128, 8, T2])
        RC = sb("RC", [128, 16]); RS = sb("RS", [128, 16])
        qkraw = sb("qkraw", [128, 512])

        yt = sb("yt", [128, 4, T2]); gl1 = sb("gl1", [128, 4, T2]); gl2 = sb("gl2", [128, 4, T2])
        ttsgt = sb("ttsgt", [128, 1024])
        tt = ttsgt[:, 0:512]
        sgt = ttsgt[:, 512:1024]
        qkT32 = ttsgt[:, :].rearrange("p (a b) -> p a b", a=8)
        qk32 = retn
        NRING = 4
        ring = [sb("ring%d" % i, [128, 8, 512], BF16) for i in range(NRING)]
        tok1 = sb("tok1", [128, 1])
        pT = [ps("pT%d" % i, [128, 1024], BF16) for i in range(2)]
        pA = [ps("pA%d" % i, [128, 512]) for i in range(6)]

        es.enter_context(nc.Block())
        P = Prog(nc, es)
        st = {"pt": 0, "pa": 0, "ring": 0, "ps": 0}

        def next_pT():
            i = st["pt"]; st["pt"] = (i + 1) % 2
            return pT[i], "pT%d" % i

        def next_pA():
            i = st["pa"]; st["pa"] = (i + 1) % 4
            return pA[i], "pA%d" % i

        def next_pS():
            i = 4 + st["ps"]; st["ps"] = (st["ps"] + 1) % 2
            return pA[i], "pA%d" % i

        def pool(meth, r, w, **kw):
            return P.add("pool", meth, r=r, w=w, **kw)

        def load(dst, src, key, eng="sp"):
            P.add(eng, "dma_start", w=[key], dma="c_" + key, out=dst, in_=src)

        load(ident_f[:], identf_d[:, :], "ident_f")
        load(maskT[:], maskT_d[:, :], "maskT")
        load(qscale[:], qs_d[:, :], "qscale")
        load(kscale[:], ks_d[:, :], "kscale")
        load(invf[:], invf_d[:, :], "invf")
        load(gpreT[:], gpreT_d[:, :], "gpreT")
        load(gffnT[:], gffnT_d[:, :], "gffnT")
        load(gpost[:], gpost_d.partition_broadcast(128), "gpost")
        load(gfpost[:], gfpost_d.partition_broadcast(128), "gfpost")
        load(pos_i[:], pos_d[:, :], "pos_i")
        load(a_re[:], are_d[:, :], "a_re"); load(a_im[:], aim_d[:, :], "a_im"); load(ldt[:], ldt_d[:, :], "ldt")
        load(b_re[:], bre_d[:, :, :], "b_re"); load(b_im[:], bim_d[:, :, :], "b_im")
        load(ct_re[:], cre_d[:, :, :], "ct_re"); load(ct_im[:], cim_d[:, :, :], "ct_im")
        load(d_T[:], dT_d[:, :], "d_T")

        def dve(meth, r, w, **kw):
            return P.add("dve", meth, r=r, w=w, **kw)

        def act(r, w, out, in_, func, **kw):
            return P.add("act", "activation", r=r, w=w, out=out, in_=in_, func=func, **kw)

        dve("tensor_copy", ["ident_f"], ["ident_b"], out=ident_b[:], in_=ident_f[:])
        dve("tensor_copy", ["pos_i"], ["pos_f"], out=pos_f[:], in_=pos_i[:])
        dve("memset", [], ["Rs"], ap=Rs[:], constant=0.0)
        dve("memset", [], ["R_bf"], ap=R_bf[:], constant=0.0)
        dve("memset", [], ["car0", "car1"], ap=car_re[:], constant=0.0)
        dve("memset", [], ["car0", "car1"], ap=car_im[:], constant=0.0)

        dt_, dare, daim, mag, sn, cs, abr, abi, den, fre, fim, tmp = sm

        def sincos(n, theta_ap, theta_keys, sin_out, cos_out, out_keys):
            A, K_, R_, B_ = ang[:, 0:n], kf[:, 0:n], rr[:, 0:n], ab[:, 0:n]
            dve("tensor_scalar", theta_keys, ["sc_k"], out=K_, in0=theta_ap, scalar1=1.0 / TWO_PI, scalar2=MAGIC, op0=ALU.mult, op1=ALU.add)
            dve("tensor_scalar", ["sc_k"], ["sc_k"], out=K_, in0=K_, scalar1=-MAGIC, scalar2=None, op0=ALU.add)
            dve("scalar_tensor_tensor", ["sc_k"] + theta_keys, ["sc_r"], out=R_, in0=K_, scalar=-C1, in1=theta_ap, op0=ALU.mult, op1=ALU.add)
            dve("scalar_tensor_tensor", ["sc_k", "sc_r"], ["sc_r"], out=R_, in0=K_, scalar=-C2, in1=R_, op0=ALU.mult, op1=ALU.add)
            dve("tensor_scalar", ["sc_r"], ["sc_r"], out=R_, in0=R_, scalar1=math.pi, scalar2=-math.pi, op0=ALU.min, op1=ALU.max)
            dve("scalar_tensor_tensor", ["sc_r"], ["sc_b"], out=B_, in0=R_, scalar=-1.0, in1=R_, op0=ALU.mult, op1=ALU.max)
            act(["sc_r"], out_keys, sin_out, R_, AF.Sin)
            act(["sc_b", "halfpi"], out_keys, cos_out, B_, AF.Sin, scale=-1.0, bias=halfpi[:, 0:1])

        dve("memset", [], ["halfpi"], ap=halfpi[:], constant=math.pi / 2)
        dve("memset", [], ["epst"], ap=epst[:], constant=EPS)

        def rsqrt(out_ap, in_ap, scale, rkeys, wkeys):
            act(rkeys + ["epst"], wkeys, out_ap, in_ap, AF.Sqrt, scale=scale, bias=epst[:, 0:1])
            dve("reciprocal", wkeys, wkeys, out=out_ap, in_=out_ap)
        act(["ldt"], ["dt"], dt_[:], ldt[:], AF.Exp)
        dve("tensor_tensor", ["dt", "a_re"], ["dare"], out=dare[:], in0=dt_[:], in1=a_re[:], op=ALU.mult)
        dve("tensor_tensor", ["dt", "a_im"], ["daim"], out=daim[:], in0=dt_[:], in1=a_im[:], op=ALU.mult)
        act(["dare"], ["mag"], mag[:], dare[:], AF.Exp)
        sincos(16, daim[:], ["daim"], sn[:], cs[:], ["sncs"])
        for nm_, ap_, k_ in [("d_daim", daim[:], ["daim"]), ("d_kf", kf[:, 0:16], ["sc_k"]), ("d_rr", rr[:, 0:16], ["sc_r"]), ("d_ab", ab[:, 0:16], ["sc_b"]), ("d_sn", sn[:], ["sncs"]), ("d_cs", cs[:], ["sncs"])]:
            if nm_ in dbg_d:
                P.add("pool", "dma_start", r=k_, dma="dbg_" + nm_, out=dbg_d[nm_], in_=ap_)
        SN, CS = "sncs", "sncs"
        dve("tensor_tensor", ["mag", CS], ["abr"], out=abr[:], in0=mag[:], in1=cs[:], op=ALU.mult)
        dve("tensor_tensor", ["mag", SN], ["abi"], out=abi[:], in0=mag[:], in1=sn[:], op=ALU.mult)
        dve("tensor_tensor", ["a_re"], ["den"], out=den[:], in0=a_re[:], in1=a_re[:], op=ALU.mult)
        dve("tensor_tensor", ["a_im"], ["tmp"], out=tmp[:], in0=a_im[:], in1=a_im[:], op=ALU.mult)
        dve("tensor_tensor", ["den", "tmp"], ["den"], out=den[:], in0=den[:], in1=tmp[:], op=ALU.add)
        dve("reciprocal", ["den"], ["den"], out=den[:], in_=den[:])
        dve("tensor_scalar", ["abr"], ["numre"], out=dare[:], in0=abr[:], scalar1=-1.0, scalar2=None, op0=ALU.add)
        dve("tensor_tensor", ["numre", "a_re"], ["fre"], out=fre[:], in0=dare[:], in1=a_re[:], op=ALU.mult)
        dve("tensor_tensor", ["abi", "a_im"], ["tmp"], out=tmp[:], in0=abi[:], in1=a_im[:], op=ALU.mult)
        dve("tensor_tensor", ["fre", "tmp"], ["fre"], out=fre[:], in0=fre[:], in1=tmp[:], op=ALU.add)
        dve("tensor_tensor", ["fre", "den"], ["fre"], out=fre[:], in0=fre[:], in1=den[:], op=ALU.mult)
        dve("tensor_tensor", ["abi", "a_re"], ["fim"], out=fim[:], in0=abi[:], in1=a_re[:], op=ALU.mult)
        dve("tensor_tensor", ["numre", "a_im"], ["tmp"], out=tmp[:], in0=dare[:], in1=a_im[:], op=ALU.mult)
        dve("tensor_tensor", ["fim", "tmp"], ["fim"], out=fim[:], in0=fim[:], in1=tmp[:], op=ALU.subtract)
        dve("tensor_tensor", ["fim", "den"], ["fim"], out=fim[:], in0=fim[:], in1=den[:], op=ALU.mult)
        freb = fre[:].unsqueeze(2).to_broadcast([128, 16, 16])
        fimb = fim[:].unsqueeze(2).to_broadcast([128, 16, 16])
        dve("tensor_tensor", ["fre", "b_re"], ["bt1"], out=bt1[:], in0=b_re[:], in1=freb, op=ALU.mult)
        dve("tensor_tensor", ["fim", "b_im"], ["bt2"], out=bt2[:], in0=b_im[:], in1=fimb, op=ALU.mult)
        dve("tensor_tensor", ["bt1", "bt2"], ["bb_re"], out=bb_re[:], in0=bt1[:], in1=bt2[:], op=ALU.subtract)
        dve("tensor_tensor", ["fre", "b_im"], ["bt1"], out=bt1[:], in0=b_im[:], in1=freb, op=ALU.mult)
        dve("tensor_tensor", ["fim", "b_re"], ["bt2"], out=bt2[:], in0=b_re[:], in1=fimb, op=ALU.mult)
        dve("tensor_tensor", ["bt1", "bt2"], ["bb_im"], out=bb_im[:], in0=bt1[:], in1=bt2[:], op=ALU.add)
        dve("memset", [], ["CC"], ap=CC[:], constant=0.0)
        for pi in range(16):
            pl = pi % 4
            dve("memset", [], ["bexp"], ap=bexp[:], constant=0.0)
            for g2 in range(2):
                c0 = pl * 32 + g2 * 16
                prt = slice(64 * g2, 64 * g2 + 64)
                dve("tensor_copy", ["bb_re"], ["bexp"], out=bexp[prt, 0, c0:c0 + 16], in_=bb_re[prt, pi, :])
                dve("tensor_copy", ["bb_im"], ["bexp"], out=bexp[prt, 1, c0:c0 + 16], in_=bb_im[prt, pi, :])
                dve("tensor_copy", ["ct_re", "CC"], ["CC"], out=CC[prt, pi, 0, c0:c0 + 16], in_=ct_re[prt, pi, :])
                dve("tensor_scalar", ["ct_im", "CC"], ["CC"], out=CC[prt, pi, 1, c0:c0 + 16], in0=ct_im[prt, pi, :], scalar1=-1.0, scalar2=None, op0=ALU.mult)
            pa, pk = next_pA()
            for ri in range(2):
                P.add("pe", "transpose", r=["bexp", "ident_f"], w=[pk], out=pa[:, ri * 128:(ri + 1) * 128], in_=bexp[:, ri, :], identity=ident_f[:])
            dve("tensor_copy", [pk], ["Lbb"], out=Lbb[:, pi, :, :], in_=pa[:, 0:256].rearrange("p (a b) -> p a b", a=2))
        dve("tensor_copy", [CS], ["tab"], out=Ctab[:, :, 0], in_=cs[:])
        dve("tensor_copy", [SN], ["tab"], out=Stab[:, :, 0], in_=sn[:])
        m = 1
        while m < T2:
            cm = Ctab[:, :, m - 1:m].to_broadcast([128, 16, m])
            smb = Stab[:, :, m - 1:m].to_broadcast([128, 16, m])
            t1 = bt1[:, :, 0:m] if m <= 16 else None
            ta = R0[:, :, 0:m]
            tb = R0[:, :, m:2 * m] if 2 * m <= T2 else None
            dve("tensor_tensor", ["tab"], ["ta"], out=ta, in0=Ctab[:, :, 0:m], in1=cm, op=ALU.mult)
            dve("tensor_tensor", ["tab"], ["tab2"], out=Ctab[:, :, m:2 * m], in0=Stab[:, :, 0:m], in1=smb, op=ALU.mult)
            dve("tensor_tensor", ["ta", "tab2"], ["tab2"], out=Ctab[:, :, m:2 * m], in0=ta, in1=Ctab[:, :, m:2 * m], op=ALU.subtract)
            dve("tensor_tensor", ["tab"], ["ta"], out=ta, in0=Stab[:, :, 0:m], in1=cm, op=ALU.mult)
            dve("tensor_tensor", ["tab"], ["tab3"], out=Stab[:, :, m:2 * m], in0=Ctab[:, :, 0:m], in1=smb, op=ALU.mult)
            dve("tensor_tensor", ["ta", "tab3"], ["tab3"], out=Stab[:, :, m:2 * m], in0=ta, in1=Stab[:, :, m:2 * m], op=ALU.add)
            dve("tensor_copy", ["tab2", "tab3", "tab"], ["tab"], out=tok1[:], in_=tok1[:])
            m *= 2
        dve("tensor_copy", ["mag"], ["rdec"], out=rdec[:], in_=mag[:])
        dve("tensor_copy", ["mag", "ta", "tab"], ["R0"], out=R0[:], in_=mag[:].unsqueeze(2).to_broadcast([128, 16, T2]))
        dve("memset", ["R0"], ["R0"], ap=R0[:, :, 0:1], constant=0.0)
        dve("tensor_tensor", ["mag", "tab"], ["RC"], out=RC[:], in0=mag[:], in1=Ctab[:, :, T2 - 1], op=ALU.mult)
        dve("tensor_tensor", ["mag", "tab"], ["RS"], out=RS[:], in0=mag[:], in1=Stab[:, :, T2 - 1], op=ALU.mult)

        dump_later = [("Ctab", Ctab[:].rearrange("p a b -> p (a b)"), ["tab"]), ("Stab", Stab[:].rearrange("p a b -> p (a b)"), ["tab"]),
                      ("R0", R0[:].rearrange("p a b -> p (a b)"), ["R0"]), ("bb_re", bb_re[:].rearrange("p a b -> p (a b)"), ["bb_re"]),
                      ("Lbb", Lbb[:].rearrange("p a b c -> p (a b c)"), ["Lbb"]), ("CC", CC[:].rearrange("p a b c -> p (a b c)"), ["CC"]),
                      ("fre", fre[:], ["fre"]), ("mag", mag[:], ["mag"]), ("sn", sn[:], ["sncs"]), ("cs", cs[:], ["sncs"])]
        dve("memset", ["Lbb", "CC", "tab", "R0", "RC", "RS", "bb_re", "bb_im", "bexp", "ct_re", "ct_im", "b_re", "b_im", "bt1", "bt2"], ["setup_done"], ap=tok1[:], constant=0.0)
        def wsrc(w_d, k0, nk, c0, ncol):
            return w_d[k0 * 128:(k0 + nk) * 128, c0:c0 + ncol].rearrange("(kc p) n -> p kc n", p=128)

        units = []
        for b in range(NB):
            for u in (WIN_ORDER if b == 0 else WIN_ORDER[1:]):
                units.append(("win%d" % u, wsrc(w_in_d, 0, 8, u * 512, 512), 8, 512))
            for u in range(2):
                units.append(("ru%d" % u, wsrc(w_ru_d, 0, 8, u * 512, 512), 8, 512))
            for u in range(2):
                units.append(("gv%d" % u, wsrc(w_gv_d, 0, 4, u * 512, 512), 4, 512))
                units.append(("gg%d" % u, wsrc(w_gg_d, 0, 4, u * 512, 512), 4, 512))
            for u in range(2):
                units.append(("wo%d" % u, wsrc(w_out_d, 0, 8, u * 512, 512), 8, 512))
            if b + 1 < NB:
                units.append(("win6", wsrc(w_in_d, 0, 8, 6 * 512, 512), 8, 512))
            for u in range(6):
                nco = 512 if u < 5 else 256
                units.append(("fg%d" % u, wsrc(w_fg_d, 0, 8, u * 512, nco), 8, nco))
                units.append(("fu%d" % u, wsrc(w_fu_d, 0, 8, u * 512, nco), 8, nco))
            for cb in range(2):
                for kg in range(3):
                    nk = 8 if kg < 2 else 6
                    units.append(("fd%d_%d" % (cb, kg), wsrc(w_fd_d, kg * 8, nk, cb * 512, 512), nk, 512))
        ust = {"issued": 0, "cur": 0, "done": 0}

        def issue_upto(n):
            n = min(n, len(units), ust["done"] + NRING)
            while ust["issued"] < n:
                i = ust["issued"]
                nm, src, nk, nco = units[i]
                s = i % NRING
                P.add("pool", "dma_start", w=["ring%d" % s], dma="ring%d" % s, out=ring[s][:, 0:nk, 0:nco], in_=src)
                ust["issued"] += 1

        def get_unit(expect):
            i = ust["cur"]
            nm = units[i][0]
            assert nm == expect, (nm, expect)
            issue_upto(i + 1)
            assert ust["issued"] > i, "ring overflow"
            ust["cur"] += 1
            s = i % NRING
            return ring[s], "ring%d" % s

        def release(n=1):
            ust["done"] += n
            issue_upto(ust["done"] + NRING)

        def dump(name, ap, keys):
            if name in dbg_d:
                P.add("pool", "dma_start", r=keys, dma="dbg_" + name, out=dbg_d[name], in_=ap)

        for nm_, ap_, k_ in dump_later:
            dump(nm_, ap_, k_)

        def phase_token(wkeys):
            dve("memset", [], wkeys, ap=tok1[:], constant=0.0)

        def rms_prep(i, src_ap, src_key, gT, gkey, dst, dkey):
            dve("memset", [], ["ss0"], ap=ss[:, 0:1], constant=0.0)
            act([src_key], ["ss0", "gated"], junk[:], src_ap, AF.Square, accum_out=ss[:, 0:1])
            rsqrt(rinv[:, 0:1], ss[:, 0:1], 1.0 / D, ["ss0"], ["rinv0"])
            act([src_key, "rinv0"], ["xn"], xn[:], src_ap, AF.Copy, scale=rinv[:, 0:1])
            pt, pk = next_pT()
            for kc in range(8):
                P.add("pe", "transpose", r=["xn", "ident_b"], w=[pk], out=pt[:, kc * 128:(kc + 1) * 128], in_=xn[:, kc * 128:(kc + 1) * 128], identity=ident_b[:])
            dve("tensor_tensor", [pk, gkey], ["%s.%d" % (dkey, i)], out=dst[:, :, i * 128:(i + 1) * 128],
                in0=pt[:, :].rearrange("p (a b) -> p a b", a=8), in1=gT[:].unsqueeze(2).to_broadcast([128, 8, 128]), op=ALU.mult)

        ALLACT = ["actT.%d" % i for i in range(4)]
        ALLU = ["uT.%d" % i for i in range(4)]
        xtmp = retn

        def stage_A_norm(bn, vt0=None):
            for i in range(4):
                gt = bn * 4 + i
                if vt0 is not None:
                    P.override = [vt0 + i * 9.0, 1e-7]
                    P.pe_delay = 7.0
                P.add("sp", "dma_start", r=(["setup_done"] if bn == 0 else []), w=["retn"], dma="xt", out=xtmp[:], in_=x_d[gt * 128:(gt + 1) * 128, :])
                rms_prep(i, xtmp[:], "retn", gpreT, "gpreT", uT, "uT")
                dve("tensor_scalar", ["pos_f", "invf"], ["ang2"], out=ang2[:], in0=invf[:], scalar1=pos_f[:, gt:gt + 1], scalar2=None, op0=ALU.mult)
                sincos(64, ang2[:], ["ang2"], cs_t[:, i, 1, :], cs_t[:, i, 0, :], ["cs.%d" % i])
            P.override = None
            P.pe_delay = 0.0

        for b in range(NB):
            for i in range(4):
                gt = b * 4 + i
                P.add("sp", "dma_start", r=(["setup_done"] if b == 0 else []), w=["x.%d" % i], dma="x%d" % i, out=x_sb[:, i, :], in_=x_d[gt * 128:(gt + 1) * 128, :])
            if b == 0:
                stage_A_norm(0)
            dump("actT", uT[:].rearrange("p a b -> p (a b)"), ALLU)

            phase_token(["R1F", "R1M"])
            phase_token(["R2S", "R2G"])
            s5vt = [None, 0.0]

            def ssm_unit():
                rg, rk = get_unit("win6")
                for ta in range(4):
                    pa, pk = next_pA()
                    for kc in range(8):
                        P.add("pe", "matmul", r=[rk] + ALLU, w=[pk], out=pa[:, :], lhsT=rg[:, kc, ta * 128:(ta + 1) * 128], rhs=uT[:, kc, :], start=(kc == 0), stop=(kc == 7))
                    act([pk], ["ussmT"], ussmT[:, ta, :], pa[:, :], AF.Copy)
                release()

            def s5_all(bidx, vt_a, vt_b):
                span = (vt_b - vt_a) * 0.92 / 8.0
                for f in range(8):
                    base = vt_a + f * span
                    s5vt[0], s5vt[1] = base, span
                    s5_part1(f, bidx)
                    s5_part2(f)
                    P.override = [base + S5_P3_DELAY * span, 1e-7]
                    s5_part3(f, bidx)
                P.override = None
                s5vt[0] = None

            def s5_part1(hc, bidx):
                t0 = hc * T2
                for half in range(2):
                    hs = slice(half * 8, half * 8 + 8)
                    wr, wi = (w_re, w_im) if half == 0 else (w_re2, w_im2)
                    wk = "w%d" % half
                    for pgl in range(2):
                        pg = half * 2 + pgl
                        if s5vt[0] is not None:
                            P.override = [s5vt[0] + pg * 0.15 * s5vt[1], 1e-7]
                        pb, pbk = next_pS()
                        pbv = pb[:, :].rearrange("p (a r t) -> p a r t", a=4, r=2)
                        for pl in range(4):
                            pi = pg * 4 + pl
                            for ri in range(2):
                                P.add("pe", "matmul", r=["Lbb", "ussmT"], w=[pbk], out=pbv[:, pl, ri, :], lhsT=Lbb[:, pi, ri, :], rhs=ussmT[:, pg, t0:t0 + T2], start=True, stop=True)
                        a_ = pbv[:, :, 0, :]
                        b_ = pbv[:, :, 1, :]
                        Cc = Ctab[:, pg * 4:pg * 4 + 4, :]
                        Sc = Stab[:, pg * 4:pg * 4 + 4, :]
                        sl = slice(pgl * 4, pgl * 4 + 4)
                        dve("tensor_tensor", [pbk, "tab"], ["s1"], out=s1[:, sl, :], in0=a_, in1=Cc, op=ALU.mult)
                        dve("tensor_tensor", [pbk, "tab"], ["s2"], out=s2[:, sl, :], in0=b_, in1=Sc, op=ALU.mult)
                        dve("tensor_tensor", ["s1", "s2"], [wk + "re"], out=wr[:, sl, :], in0=s1[:, sl, :], in1=s2[:, sl, :], op=ALU.add)
                        dve("tensor_tensor", [pbk, "tab"], ["s1"], out=s1[:, sl, :], in0=b_, in1=Cc, op=ALU.mult)
                        dve("tensor_tensor", [pbk, "tab"], ["s2"], out=s2[:, sl, :], in0=a_, in1=Sc, op=ALU.mult)
                        dve("tensor_tensor", ["s1", "s2"], [wk + "im"], out=wi[:, sl, :], in0=s1[:, sl, :], in1=s2[:, sl, :], op=ALU.subtract)
                    dve("tensor_tensor", [wk + "re", "car%d" % half], [wk + "re"], out=wr[:, :, 0], in0=wr[:, :, 0], in1=car_re[:, hs], op=ALU.add)
                    dve("tensor_tensor", [wk + "im", "car%d" % half], [wk + "im"], out=wi[:, :, 0], in0=wi[:, :, 0], in1=car_im[:, hs], op=ALU.add)
                    R0h = R0[:, hs, :].rearrange("p a b -> p (a b)")
                    dve("tensor_tensor_scan", [wk + "re", "R0"], [wk + "re"], out=wr[:].rearrange("p a b -> p (a b)"), data0=R0h, data1=wr[:].rearrange("p a b -> p (a b)"), initial=0.0, op0=ALU.mult, op1=ALU.add)
                    dve("tensor_tensor_scan", [wk + "im", "R0"], [wk + "im"], out=wi[:].rearrange("p a b -> p (a b)"), data0=R0h, data1=wi[:].rearrange("p a b -> p (a b)"), initial=0.0, op0=ALU.mult, op1=ALU.add)
                    vr, vi = wr[:, :, T2 - 1], wi[:, :, T2 - 1]
                    dve("tensor_tensor", [wk + "re", "RC"], ["cl0"], out=cl[0][:], in0=vr, in1=RC[:, hs], op=ALU.mult)
                    dve("tensor_tensor", [wk + "im", "RS"], ["cl1"], out=cl[1][:], in0=vi, in1=RS[:, hs], op=ALU.mult)
                    dve("tensor_tensor", [wk + "im", "RC"], ["cl2"], out=cl[2][:], in0=vi, in1=RC[:, hs], op=ALU.mult)
                    dve("tensor_tensor", [wk + "re", "RS"], ["cl3"], out=cl[3][:], in0=vr, in1=RS[:, hs], op=ALU.mult)
                    dve("tensor_tensor", ["cl0", "cl1"], ["car%d" % half], out=car_re[:, hs], in0=cl[0][:], in1=cl[1][:], op=ALU.subtract)
                    dve("tensor_tensor", ["cl2", "cl3"], ["car%d" % half], out=car_im[:, hs], in0=cl[2][:], in1=cl[3][:], op=ALU.add)
                    if bidx == 0 and hc == 0 and half == 0:
                        dump("vre", wr[:].rearrange("p a b -> p (a b)"), [wk + "re"])

            def s5_part2(hc):
                for half in range(2):
                    hs = slice(half * 8, half * 8 + 8)
                    wr, wi = (w_re, w_im) if half == 0 else (w_re2, w_im2)
                    wk = "w%d" % half
                    Ch = Ctab[:, hs, :]
                    Sh = Stab[:, hs, :]
                    dve("tensor_tensor", [wk + "re", "tab"], ["s1"], out=s1[:], in0=wr[:], in1=Ch, op=ALU.mult)
                    dve("tensor_tensor", [wk + "im", "tab"], ["s2"], out=s2[:], in0=wi[:], in1=Sh, op=ALU.mult)
                    dve("tensor_tensor", [wk + "im", "tab"], ["p1"], out=p1[:], in0=wi[:], in1=Ch, op=ALU.mult)
                    dve("tensor_tensor", [wk + "re", "tab"], ["p2"], out=p2[:], in0=wr[:], in1=Sh, op=ALU.mult)
                    dve("tensor_tensor", ["s1", "s2"], ["xr_bf"], out=xr_bf[:, hs, :], in0=s1[:], in1=s2[:], op=ALU.subtract)
                    dve("tensor_tensor", ["p1", "p2"], ["xi_bf"], out=xi_bf[:, hs, :], in0=p1[:], in1=p2[:], op=ALU.add)

            def s5_part3(hc, bidx):
                t0 = hc * T2
                py, pyk = next_pS()
                pyv = py[:, 0:4 * T2].rearrange("p (a t) -> p a t", a=4)
                for ta in range(4):
                    for pl in range(4):
                        pi = ta * 4 + pl
                        P.add("pe", "matmul", r=["CC", "xr_bf"], w=[pyk], out=pyv[:, ta, :], lhsT=CC[:, pi, 0, :], rhs=xr_bf[:, pi, :], start=(pl == 0), stop=False)
                        P.add("pe", "matmul", r=["CC", "xi_bf"], w=[pyk], out=pyv[:, ta, :], lhsT=CC[:, pi, 1, :], rhs=xi_bf[:, pi, :], start=False, stop=(pl == 3))
                for ta in range(4):
                    dve("scalar_tensor_tensor", [pyk, "ussmT", "d_T"], ["yt"], out=yt[:, ta, :], in0=ussmT[:, ta, t0:t0 + T2], scalar=d_T[:, ta:ta + 1], in1=pyv[:, ta, :], op0=ALU.mult, op1=ALU.add)
                if bidx == 0 and hc == 0:
                    dump("yt", yt[:].rearrange("p a b -> p (a b)"), ["yt"])
                pool("tensor_tensor", ["yt"], ["g1"], out=gl1[:], in0=yt[:], in1=yt[:], op=ALU.mult)
                pool("tensor_scalar", ["g1"], ["g1"], out=gl1[:], in0=gl1[:], scalar1=0.044715, scalar2=1.0, op0=ALU.mult, op1=ALU.add)
                pool("tensor_tensor", ["g1", "yt"], ["g1"], out=gl1[:], in0=gl1[:], in1=yt[:], op=ALU.mult)
                act(["g1"], ["g2"], gl2[:], gl1[:], AF.Sigmoid, scale=1.5957691216057308)
                pool("tensor_tensor", ["g2", "yt"], ["zT"], out=zT[:, :, t0:t0 + T2], in0=gl2[:], in1=yt[:], op=ALU.mult)

            if b == 0:
                ssm_unit()
                vt_s5_start = P.clock
            for idx, u in enumerate(WIN_ORDER[1:]):
                rg, rk = get_unit("win%d" % u)
                for i in range(4):
                    pa, pk = next_pA()
                    for kc in range(8):
                        P.add("pe", "matmul", r=[rk, "uT.%d" % i], w=[pk], out=pa[:, :], lhsT=uT[:, kc, i * 128:(i + 1) * 128], rhs=rg[:, kc, :], start=(kc == 0), stop=(kc == 7))
                    if u < 2:
                        act([pk], ["qkraw"], qkraw[:], pa[:, :], AF.Copy)
                        xv = qkraw[:, :].rearrange("p (h t f) -> p h t f", h=4, t=2)
                        x1, x2 = xv[:, :, 0, :], xv[:, :, 1, :]
                        cb_ = cs_t[:, i, 0:1, :].to_broadcast([128, 4, 64])
                        sb_ = cs_t[:, i, 1:2, :].to_broadcast([128, 4, 64])
                        ck = "cs.%d" % i
                        pool("tensor_tensor", ["qkraw", ck], ["rt0"], out=rt[0][:], in0=x1, in1=cb_, op=ALU.mult)
                        pool("tensor_tensor", ["qkraw", ck], ["rt1"], out=rt[1][:], in0=x2, in1=sb_, op=ALU.mult)
                        pool("tensor_tensor", ["qkraw", ck], ["rt2"], out=rt[2][:], in0=x1, in1=sb_, op=ALU.mult)
                        pool("tensor_tensor", ["qkraw", ck], ["rt3"], out=rt[3][:], in0=x2, in1=cb_, op=ALU.mult)
                        pool("tensor_tensor", ["rt0", "rt1"], ["ro"], out=ro[:, :, 0, :], in0=rt[0][:], in1=rt[1][:], op=ALU.subtract)
                        pool("tensor_tensor", ["rt2", "rt3", "ro"], ["ro"], out=ro[:, :, 1, :], in0=rt[2][:], in1=rt[3][:], op=ALU.add)
                        scl = (qscale if u == 0 else kscale)[:].unsqueeze(2).to_broadcast([128, 4, 128])
                        pool("tensor_tensor", ["ro", "qscale", "kscale", "R1M"], ["qk.%d" % i],
                             out=qk_sb[:, i, u * 512:(u + 1) * 512].rearrange("p (h f) -> p h f", h=4),
                             in0=ro[:].rearrange("p h t f -> p h (t f)"), in1=scl, op=ALU.mult)
                        if b == 0 and i == 0:
                            pool("tensor_tensor", ["ro", "qscale", "kscale"], ["qk32", "retn"],
                                 out=qk32[:, u * 512:(u + 1) * 512].rearrange("p (h f) -> p h f", h=4),
                                 in0=ro[:].rearrange("p h t f -> p h (t f)"), in1=scl, op=ALU.mult)
                    elif u < 4:
                        act([pk, "R1M"], ["v.%d" % i], v_sb[:, i, (u - 2) * 512:(u - 1) * 512], pa[:, :], AF.Copy)
                    elif u < 6:
                        act([pk, "R1M"], ["sg.%d" % i], sg_sb[:, i, (u - 4) * 512:(u - 3) * 512], pa[:, :], AF.Silu)
                    else:
                        act([pk, "R2G"], ["gates.%d" % i], gates[:, i, (u - 7) * 512:(u - 6) * 512], pa[:, :], AF.Sigmoid)
                release()
            dump("qk", qk_sb, ["qk.%d" % i for i in range(4)])
            dump("v", v_sb, ["v.%d" % i for i in range(4)])
            dump("sg", sg_sb, ["sg.%d" % i for i in range(4)])
            dump("ussmT", ussmT[:].rearrange("p a b -> p (a b)"), ["ussmT"])

            for i in range(4):
                pt, ptk = next_pT()
                for j in range(8):
                    P.add("pe", "transpose", r=["qk.%d" % i, "ident_b", "R1M"], w=[ptk], out=pt[:, j * 128:(j + 1) * 128], in_=qk_sb[:, i, j * 128:(j + 1) * 128], identity=ident_b[:])
                act([ptk], ["qkT"], qkT[:].rearrange("p a b -> p (a b)"), pt[:, :], AF.Copy)
                if b == 0 and i == 0:
                    for hf in range(2):
                        pa32, pk32 = next_pA()
                        for jj in range(4):
                            j = hf * 4 + jj
                            P.add("pe", "transpose", r=["qk32", "retn", "ident_f"], w=[pk32], out=pa32[:, jj * 128:(jj + 1) * 128], in_=qk32[:, j * 128:(j + 1) * 128], identity=ident_f[:])
                        act([pk32], ["qkT32", "tt", "sgt"], qkT32[:, hf * 4:hf * 4 + 4, :], pa32[:, :].rearrange("p (a b) -> p a b", a=4), AF.Copy)
                psc, psk = next_pA()
                for h in range(NH):
                    if b == 0 and i == 0:
                        P.add("pe", "matmul", r=["qkT32", "tt", "sgt"], w=[psk], out=psc[:, h * 128:(h + 1) * 128], lhsT=qkT32[:, 4 + h, :], rhs=qkT32[:, h, :], start=True, stop=True)
                    else:
                        P.add("pe", "matmul", r=["qkT"], w=[psk], out=psc[:, h * 128:(h + 1) * 128], lhsT=qkT[:, 4 + h, :], rhs=qkT[:, h, :], start=True, stop=True)
                dve("tensor_tensor", [psk, "maskT"], ["sc_bf"], out=sc_bf[:], in0=psc[:, :].rearrange("p (h i) -> p h i", h=4),
                    in1=maskT[:].unsqueeze(1).to_broadcast([128, 4, 128]), op=ALU.mult)
                pr = []
                for hp in range(2):
                    pa, pk = next_pA()
                    pr.append((pa, pk))
                    for hh in range(2):
                        h = hp * 2 + hh
                        P.add("pe", "matmul", r=["sc_bf", "v.%d" % i, "R1M"], w=[pk], out=pa[:, hh * 256:(hh + 1) * 256], lhsT=sc_bf[:, h, :], rhs=v_sb[:, i, h * 256:(h + 1) * 256], start=True, stop=False)
                        P.add("pe", "matmul", r=["qkT", "R_bf"], w=[pk], out=pa[:, hh * 256:(hh + 1) * 256], lhsT=qkT[:, h, :], rhs=R_bf[:, h, :], start=False, stop=True)
                for hp in range(2):
                    pa, pk = next_pA()
                    for hh in range(2):
                        h = hp * 2 + hh
                        P.add("pe", "matmul", r=["qk.%d" % i, "v.%d" % i, "R1M"], w=[pk], out=pa[:, hh * 256:(hh + 1) * 256], lhsT=qk_sb[:, i, 512 + h * 128:512 + (h + 1) * 128], rhs=v_sb[:, i, h * 256:(h + 1) * 256], start=True, stop=True)
                    for hh in range(2):
                        h = hp * 2 + hh
                        dve("scalar_tensor_tensor", [pk, "Rs"], ["Rs"], out=Rs[:, h, :], in0=Rs[:, h, :], scalar=GAM128[h], in1=pa[:, hh * 256:(hh + 1) * 256], op0=ALU.mult, op1=ALU.add)
                        act(["Rs"], ["R_bf"], R_bf[:, h, :], Rs[:, h, :], AF.Copy, scale=GAM128[h])
                for hp in range(2):
                    pa, pk = pr[hp]
                    for hh in range(2):
                        dve("bn_stats", [pk], ["st6"], out=st6[:, hp * 2 + hh, :], in_=pa[:, hh * 256:(hh + 1) * 256])
                for h in range(NH):
                    dve("bn_aggr", ["st6"], ["mv"], out=mv[:, h, :], in_=st6[:, h, :])
                rsqrt(rstd[:], mv[:, :, 1], 1.0, ["mv"], ["rstd"])
                for hp in range(2):
                    pa, pk = pr[hp]
                    for hh in range(2):
                        h = hp * 2 + hh
                        dve("tensor_scalar", [pk, "mv", "rstd"], ["retn"], out=retn[:, h * 256:(h + 1) * 256], in0=pa[:, hh * 256:(hh + 1) * 256],
                            scalar1=mv[:, h, 0:1], scalar2=rstd[:, h:h + 1], op0=ALU.subtract, op1=ALU.mult)
                if b == 0 and i == 0:
                    dump("retn", retn[:], ["retn"])
                if b == 0:
                    dump("retn%d" % i, retn[:], ["retn"])
                dve("tensor_tensor", ["retn", "sg.%d" % i, "R1M"], ["gated"], out=gated[:], in0=retn[:], in1=sg_sb[:, i, :], op=ALU.mult)
                pt, ptk = next_pT()
                for kc in range(8):
                    P.add("pe", "transpose", r=["gated", "ident_b"], w=[ptk], out=pt[:, kc * 128:(kc + 1) * 128], in_=gated[:, kc * 128:(kc + 1) * 128], identity=ident_b[:])
                act([ptk] + ["actT.%d" % i], ["actT.%d" % i], actT[:, :, i * 128:(i + 1) * 128], pt[:, :].rearrange("p (a b) -> p a b", a=8), AF.Copy)

            dump("zT", zT[:].rearrange("p a b -> p (a b)"), ["zT"])

            vt_E1_start = P.clock
            for u in range(2):
                rg, rk = get_unit("ru%d" % u)
                for i in range(4):
                    pa, pk = next_pA()
                    for kc in range(8):
                        P.add("pe", "matmul", r=[rk, "actT.%d" % i], w=[pk], out=pa[:, :], lhsT=actT[:, kc, i * 128:(i + 1) * 128], rhs=rg[:, kc, :], start=(kc == 0), stop=(kc == 7))
                    dve("tensor_tensor", [pk, "gates.%d" % i, "R1M", "R2G"], ["mix.%d.%d" % (i, u), "sg.%d" % i], out=mix_sb[:, i, u * 512:(u + 1) * 512], in0=pa[:, :], in1=gates[:, i, u * 512:(u + 1) * 512], op=ALU.mult)
                release()
            if b == 0:
                s5_all(0, vt_s5_start, P.clock)
            dump("mixa", mix_sb, ["mix.%d.%d" % (i, u) for i in range(4) for u in range(2)])
            for u in range(2):
                rgv, rkv = get_unit("gv%d" % u)
                rgg, rkg = get_unit("gg%d" % u)
                for i in range(4):
                    pv, pvk = next_pA()
                    pg_, pgk = next_pA()
                    for kc in range(4):
                        P.add("pe", "matmul", r=[rkv, "zT"], w=[pvk], out=pv[:, :], lhsT=zT[:, kc, i * 128:(i + 1) * 128], rhs=rgv[:, kc, :], start=(kc == 0), stop=(kc == 3))
                    for kc in range(4):
                        P.add("pe", "matmul", r=[rkg, "zT"], w=[pgk], out=pg_[:, :], lhsT=zT[:, kc, i * 128:(i + 1) * 128], rhs=rgg[:, kc, :], start=(kc == 0), stop=(kc == 3))
                    act([pgk], ["sgt"], sgt[:], pg_[:, :], AF.Sigmoid)
                    dve("tensor_tensor", [pvk, "sgt"], ["tt"], out=tt[:], in0=pv[:, :], in1=sgt[:], op=ALU.mult)
                    dve("tensor_tensor", ["tt", "gates.%d" % i, "R2G"], ["tt"], out=tt[:], in0=tt[:], in1=gates[:, i, 1024 + u * 512:1024 + (u + 1) * 512], op=ALU.mult)
                    mk = "mix.%d.%d" % (i, u)
                    dve("tensor_tensor", ["tt", mk, "R1M"], [mk], out=mix_sb[:, i, u * 512:(u + 1) * 512], in0=tt[:], in1=mix_sb[:, i, u * 512:(u + 1) * 512], op=ALU.add)
                release(2)
            for i in range(4):
                pt, ptk = next_pT()
                for kc in range(8):
                    P.add("pe", "transpose", r=["mix.%d.0" % i, "mix.%d.1" % i, "ident_b", "R1M"], w=[ptk], out=pt[:, kc * 128:(kc + 1) * 128], in_=mix_sb[:, i, kc * 128:(kc + 1) * 128], identity=ident_b[:])
                act([ptk], ["actT.%d" % i], actT[:, :, i * 128:(i + 1) * 128], pt[:, :].rearrange("p (a b) -> p a b", a=8), AF.Copy)

            dump("mix", mix_sb, ["mix.%d.%d" % (i, u) for i in range(4) for u in range(2)])
            dump("gates", gates, ["gates.%d" % i for i in range(4)])
            phase_token(["R2G", "R2S"] + ["gates.%d" % i for i in range(4)])
            for u in range(2):
                rg, rk = get_unit("wo%d" % u)
                for i in range(4):
                    pa, pk = next_pA()
                    for kc in range(8):
                        P.add("pe", "matmul", r=[rk, "actT.%d" % i], w=[pk], out=pa[:, :], lhsT=actT[:, kc, i * 128:(i + 1) * 128], rhs=rg[:, kc, :], start=(kc == 0), stop=(kc == 7))
                    act([pk, "R2S"], ["stg.%d.%d" % (i, u)], stg[:, i, u * 512:(u + 1) * 512], pa[:, :], AF.Copy)
                    dve("memset", [], ["ssq.%d.%d" % (i, u)], ap=ss2[:, 2 * i + u:2 * i + u + 1], constant=0.0)
                    act(["stg.%d.%d" % (i, u), "R2S"], ["ssq.%d.%d" % (i, u), "gated"], junk[:, 0:512], stg[:, i, u * 512:(u + 1) * 512], AF.Square, accum_out=ss2[:, 2 * i + u:2 * i + u + 1])
                release()

            def post_norm_res(i, g_rep, gkey, dst_is_x):
                k0, k1 = "ssq.%d.0" % i, "ssq.%d.1" % i
                dve("tensor_tensor", [k0, k1], ["rinv1"], out=rinv[:, 1 + i:2 + i], in0=ss2[:, 2 * i:2 * i + 1], in1=ss2[:, 2 * i + 1:2 * i + 2], op=ALU.add)
                rsqrt(rinv[:, 1 + i:2 + i], rinv[:, 1 + i:2 + i], 1.0 / D, ["rinv1"], ["rinv1"])
                sk = ["stg.%d.0" % i, "stg.%d.1" % i]
                dve("scalar_tensor_tensor", sk + ["rinv1", gkey, "R2S"], sk, out=stg[:, i, :], in0=stg[:, i, :], scalar=rinv[:, 1 + i:2 + i], in1=g_rep[:], op0=ALU.mult, op1=ALU.mult)
                if dst_is_x:
                    pool("tensor_tensor", sk + ["x.%d" % i, "R2S"], ["x.%d" % i], out=x_sb[:, i, :], in0=stg[:, i, :], in1=x_sb[:, i, :], op=ALU.add)
                else:
                    pool("tensor_tensor", sk + ["x.%d" % i, "R2S"], sk, out=stg[:, i, :], in0=stg[:, i, :], in1=x_sb[:, i, :], op=ALU.add)

            for i in range(4):
                post_norm_res(i, gpost, "gpost", True)
            if b == 0:
                dump("h", x_sb[:, 0, :], ["x.0"])

            if b + 1 < NB:
                stage_A_norm(b + 1, vt_E1_start)
                ssm_unit()
                vt_s5_start = P.clock
            for i in range(4):
                rms_prep(i, x_sb[:, i, :], "x.%d" % i, gffnT, "gffnT", actT, "actT")
            phase_token(["R1M", "R1F"] + ["qk.%d" % i for i in range(4)] + ["v.%d" % i for i in range(4)] + ["sg.%d" % i for i in range(4)]
                        + ["mix.%d.%d" % (i, u) for i in range(4) for u in range(2)])
            for u in range(6):
                rgg, rkg = get_unit("fg%d" % u)
                rgu, rku = get_unit("fu%d" % u)
                nft = 4 if u < 5 else 2
                for ft in range(nft):
                    pg_, pgk = next_pA()
                    pu, puk = next_pA()
                    for kc in range(8):
                        P.add("pe", "matmul", r=[rkg] + ALLACT, w=[pgk], out=pg_[:, :], lhsT=rgg[:, kc, ft * 128:(ft + 1) * 128], rhs=actT[:, kc, :], start=(kc == 0), stop=(kc == 7))
                    for kc in range(8):
                        P.add("pe", "matmul", r=[rku] + ALLACT, w=[puk], out=pu[:, :], lhsT=rgu[:, kc, ft * 128:(ft + 1) * 128], rhs=actT[:, kc, :], start=(kc == 0), stop=(kc == 7))
                    act([pgk], ["sgt"], sgt[:], pg_[:, :], AF.Silu)
                    act([puk], ["tt"], tt[:], pu[:, :], AF.Copy)
                    pool("tensor_tensor", ["tt", "sgt", "R1F"], ["hid.%d" % (u * 4 + ft)], out=hidT[:, u * 4 + ft, :], in0=tt[:], in1=sgt[:], op=ALU.mult)
                release(2)

            vt_I_start = P.clock
            for cb in range(2):
                rgs = [get_unit("fd%d_%d" % (cb, kg)) for kg in range(3)]
                for i in range(4):
                    pa, pk = next_pA()
                    for kc in range(22):
                        rg, rk = rgs[kc // 8]
                        P.add("pe", "matmul", r=[rk, "hid.%d" % kc, "R1F"], w=[pk], out=pa[:, :], lhsT=hidT[:, kc, i * 128:(i + 1) * 128], rhs=rg[:, kc % 8, :], start=(kc == 0), stop=(kc == 21))
                    act([pk, "R2S"], ["stg.%d.%d" % (i, cb)], stg[:, i, cb * 512:(cb + 1) * 512], pa[:, :], AF.Copy)
                    dve("memset", [], ["ssq.%d.%d" % (i, cb)], ap=ss2[:, 2 * i + cb:2 * i + cb + 1], constant=0.0)
                    act(["stg.%d.%d" % (i, cb), "R2S"], ["ssq.%d.%d" % (i, cb), "gated"], junk[:, 0:512], stg[:, i, cb * 512:(cb + 1) * 512], AF.Square, accum_out=ss2[:, 2 * i + cb:2 * i + cb + 1])
                release(3)
            for i in range(4):
                gt = b * 4 + i
                post_norm_res(i, gfpost, "gfpost", False)
                P.add("sp", "dma_start", r=["stg.%d.0" % i, "stg.%d.1" % i], w=["yout.%d" % i], dma="y%d" % i, out=y_d[gt * 128:(gt + 1) * 128, :], in_=stg[:, i, :])
                P.ops[-1].r = P.ops[-1].r + ("R2S",)
            if b + 1 < NB:
                s5_all(b + 1, vt_s5_start, P.clock)

        P.finalize()
    return nc


def _consts():
    idx = np.arange(128, dtype=np.float64)
    qs = np.stack([np.array(GAM[h], np.float64) ** (idx + 1.0) for h in range(NH)], axis=1)
    ks = np.stack([np.array(GAM[h], np.float64) ** (-(idx + 1.0)) * (128.0 ** -0.5) for h in range(NH)], axis=1)
    maskT = (idx[:, None] <= idx[None, :]).astype(np.float32)
    invf = (10000.0 ** (-np.arange(0, 128, 2, dtype=np.float32) / np.float32(128))).astype(np.float32)
    return dict(
        ident_f=np.eye(128, dtype=np.float32),
        maskT=maskT,
        qscale=qs.astype(np.float32),
        kscale=ks.astype(np.float32),
        invfreq=np.ascontiguousarray(np.broadcast_to(invf[None, :], (128, 64))).astype(np.float32),
    )


def _shared_inputs(inp):
    f = lambda a: np.ascontiguousarray(np.asarray(a, dtype=np.float32))
    pair = lambda a: f(np.asarray(a).reshape(16, 2, 64).transpose(1, 2, 0).reshape(128, 16))
    m = dict(
        w_in=f(inp["w_in"][0]), w_ret_up=f(inp["w_ret_up"][0]), w_glu_val=f(inp["w_glu_val"][0]),
        w_glu_gate=f(inp["w_glu_gate"][0]), w_out=f(inp["w_out"][0]), w_ffn_gate=f(inp["w_ffn_gate"][0]),
        w_ffn_up=f(inp["w_ffn_up"][0]), w_ffn_down=f(inp["w_ffn_down"][0]),
        g_pre_T=f(np.asarray(inp["mix_pre_norm"][0]).reshape(8, 128).T),
        g_ffn_T=f(np.asarray(inp["ffn_pre_norm"][0]).reshape(8, 128).T),
        g_post=f(np.asarray(inp["mix_post_norm"][0]).reshape(1, D)),
        g_fpost=f(np.asarray(inp["ffn_post_norm"][0]).reshape(1, D)),
        a_re=pair(inp["ssm_a_re"][0]), a_im=pair(inp["ssm_a_im"][0]),
        log_dt=f(np.broadcast_to(np.asarray(inp["ssm_log_dt"][0]).reshape(16, 2, 1), (16, 2, 64)).transpose(1, 2, 0).reshape(128, 16)),
        b_re=f(np.asarray(inp["ssm_b_re"][0]).reshape(16, 2, 64, 16).transpose(1, 2, 0, 3).reshape(128, 16, 16)),
        b_im=f(np.asarray(inp["ssm_b_im"][0]).reshape(16, 2, 64, 16).transpose(1, 2, 0, 3).reshape(128, 16, 16)),
        ct_re=f(np.asarray(inp["ssm_c_re"][0]).reshape(16, 2, 16, 64).transpose(1, 3, 0, 2).reshape(128, 16, 16)),
        ct_im=f(np.asarray(inp["ssm_c_im"][0]).reshape(16, 2, 16, 64).transpose(1, 3, 0, 2).reshape(128, 16, 16)),
        d_T=f(np.asarray(inp["ssm_d"][0]).reshape(4, 128).T),
    )
    m.update(_consts())
    return m


def make_in_maps(inp, ntok, cores):
    shared = _shared_inputs(inp)
    maps = []
    x = np.asarray(inp["x"])
    pos = np.asarray(inp["positions"])
    for c in cores:
        mm = dict(shared)
        mm["x"] = np.ascontiguousarray(x[c, :ntok, :], dtype=np.float32)
        mm["pos"] = np.ascontiguousarray(pos[c, :ntok].reshape(ntok // 128, 128).T).astype(np.int32)
        maps.append(mm)
    return maps


def kernel(**inputs):
    nc = build(SEQ)
    cores = list(range(8))
    in_maps = make_in_maps(inputs, SEQ, cores)
    res = run_bass_kernel_spmd(nc, in_maps, core_ids=cores)
    out = np.stack([np.asarray(r["y"]) for r in res.results], axis=0)
    return out.astype(np.float32)
```
